# Optimizing a Trainium2 kernel written in Bass

```python
import jax
import jax.numpy as jnp
from jax import lax
import numpy as np

D_MODEL = 2048
BATCH = 2
SEQ = 4096
DEPTH = 2
DEC_BATCH = 32
DEC_SEQ = 8
PAST_LEN = 8192
PAGE_SIZE = 128

HEAD_DIM = 128
N_EVEN = (DEPTH + 1) // 2
N_ODD = DEPTH // 2
NSA_HEADS = D_MODEL // (2 * HEAD_DIM)
NSA_KV_HEADS = NSA_HEADS // 4
NSA_GROUP = NSA_HEADS // NSA_KV_HEADS
CMP_BLOCK = 64
SEL_BLOCK = 64
SEL_TOPK = 16
WINDOW = 512
CMP_HIDDEN = 256
RET_HEADS = D_MODEL // (2 * HEAD_DIM)
RET_DK = HEAD_DIM
RET_DV = HEAD_DIM
RET_CHUNK = 128
FOX_HEADS = D_MODEL // HEAD_DIM
FOX_KV_HEADS = FOX_HEADS // 4
FOX_GROUP = FOX_HEADS // FOX_KV_HEADS
Q_BLOCK = 128
PEER_HEADS = 8
PEER_DK = 256
N_KEYS = 128
N_EXPERTS = N_KEYS * N_KEYS
PEER_TOPK = 16
PEER_CHUNK = 128
EPS = 1e-6
GN_EPS = 1e-5
NEG = -1e30

kernel_name = 'nsa_retnet_fox_peer_decode_step'


def ab_sizes():
    qa = NSA_HEADS * HEAD_DIM
    kv = NSA_KV_HEADS * HEAD_DIM
    return (qa, kv, kv, kv, kv, kv, kv, NSA_HEADS * 3,
            RET_HEADS * RET_DK, RET_HEADS * RET_DK, RET_HEADS * RET_DV, RET_HEADS * RET_DV)


def c_sizes():
    return (FOX_HEADS * HEAD_DIM, FOX_KV_HEADS * HEAD_DIM, FOX_KV_HEADS * HEAD_DIM, FOX_HEADS)


def rmsnorm(x, w):
    xf = x.astype(jnp.float32)
    y = xf * lax.rsqrt(jnp.mean(xf * xf, axis=-1, keepdims=True) + EPS) * w.astype(jnp.float32)
    return y.astype(x.dtype)


def split_cols(h, sizes):
    out = []
    start = 0
    for n in sizes:
        out.append(h[..., start:start + n])
        start += n
    return out


def masked_softmax(s, mask):
    s = jnp.where(mask, s.astype(jnp.float32), NEG)
    m = jnp.max(s, axis=-1, keepdims=True)
    e = jnp.where(mask, jnp.exp(s - m), 0.0)
    return e / jnp.maximum(jnp.sum(e, axis=-1, keepdims=True), 1e-30)


def alibi_slopes():
    return jnp.exp2(-8.0 * (jnp.arange(NSA_HEADS, dtype=jnp.float32) + 1.0) / NSA_HEADS)


def compress_blocks(blocks, pe, w1, w2):
    w1r = w1.reshape(CMP_BLOCK, HEAD_DIM, CMP_HIDDEN)
    h = jax.nn.gelu(jnp.einsum('bnlgd,ldh->bngh', blocks + pe[None, None, :, None, :], w1r))
    return jnp.einsum('bngh,he->bnge', h, w2)


def nsa_attend(q, gates, rows, win_arr, offset, wstart, pe_k, w1_k, w2_k, pe_v, w1_v, w2_v):
    B, Tq, H, hd = q.shape
    Tk = rows.shape[1]
    G, R = NSA_KV_HEADS, NSA_GROUP
    nb = Tk // CMP_BLOCK
    blocks = rows[:, :nb * CMP_BLOCK].reshape(B, nb, CMP_BLOCK, 4, G, hd)
    kc = compress_blocks(blocks[:, :, :, 0], pe_k, w1_k, w2_k)
    vc = compress_blocks(blocks[:, :, :, 1], pe_v, w1_v, w2_v)
    n_sel = -(-Tk // SEL_BLOCK)
    sel = jnp.pad(rows[:, :, 2:4], ((0, 0), (0, n_sel * SEL_BLOCK - Tk), (0, 0), (0, 0), (0, 0)))
    sel = sel.reshape(B, n_sel, SEL_BLOCK, 2, G, hd).transpose(0, 4, 1, 2, 3, 5)
    k_top = min(SEL_TOPK, n_sel)
    slopes = alibi_slopes().reshape(G, R)
    qb = Q_BLOCK if Tq % Q_BLOCK == 0 else Tq
    nq = Tq // qb
    qs = (q * hd ** -0.5).reshape(B, nq, qb, G, R, hd).transpose(1, 0, 2, 3, 4, 5)
    gs = jax.nn.sigmoid(gates.astype(jnp.float32)).reshape(B, nq, qb, G, R, 3).transpose(1, 0, 2, 3, 4, 5)
    bi = jnp.arange(B)[:, None, None, None]
    gi = jnp.arange(G)[None, :, None, None]
    blk_end = jnp.arange(nb) * CMP_BLOCK + (CMP_BLOCK - 1)
    jsel = jnp.arange(n_sel)
    in_blk = jnp.arange(SEL_BLOCK)

    def one_block(xs):
        qg, g, i = xs
        a = offset + i * qb
        t = a + jnp.arange(qb)
        d_c = t[:, None] - blk_end[None, :]
        s = jnp.einsum('bqgrd,bngd->bgrqn', qg, kc).astype(jnp.float32) - slopes[:, :, None, None] * d_c
        p_c = masked_softmax(s, d_c >= 0)
        o_c = jnp.einsum('bgrqn,bngd->bqgrd', p_c, vc)
        imp = jnp.pad(p_c.sum(axis=2), ((0, 0), (0, 0), (0, 0), (0, n_sel - nb)))
        cur = t // SEL_BLOCK
        forced = (jsel[None, :] == 0) | (jsel[None, :] == cur[:, None]) | (jsel[None, :] == cur[:, None] - 1)
        score = jnp.where(forced, R + 1.0, jnp.where(jsel[None, :] <= cur[:, None], imp, -1.0))
        vals, idx = lax.top_k(score, k_top)
        kv_sel = sel[bi, gi, idx]
        pos = idx[..., None] * SEL_BLOCK + in_blk
        d_s = t[None, None, :, None, None] - pos
        m_s = (vals >= 0)[..., None] & (d_s >= 0)
        s = jnp.einsum('bqgrd,bgqkld->bgrqkl', qg, kv_sel[..., 0, :]).astype(jnp.float32)
        s = s - slopes[None, :, :, None, None, None] * d_s[:, :, None]
        p_s = masked_softmax(s.reshape(B, G, R, qb, k_top * SEL_BLOCK),
                             m_s[:, :, None].reshape(B, G, 1, qb, k_top * SEL_BLOCK))
        o_s = jnp.einsum('bgrqkl,bgqkld->bqgrd', p_s.reshape(B, G, R, qb, k_top, SEL_BLOCK), kv_sel[..., 1, :])
        kv_w = lax.dynamic_slice_in_dim(win_arr, a - wstart, WINDOW + qb, axis=1)
        pos_w = a - WINDOW + jnp.arange(WINDOW + qb)
        d_w = t[:, None] - pos_w[None, :]
        m_w = (d_w >= 0) & (d_w <= WINDOW) & (pos_w >= wstart)[None, :]
        s = jnp.einsum('bqgrd,bkgd->bgrqk', qg, kv_w[:, :, 0]).astype(jnp.float32) - slopes[:, :, None, None] * d_w
        p_w = masked_softmax(s, m_w)
        o_w = jnp.einsum('bgrqk,bkgd->bqgrd', p_w, kv_w[:, :, 1])
        o = g[..., 0:1] * o_c + g[..., 1:2] * o_s + g[..., 2:3] * o_w
        return o.reshape(B, qb, H, hd).astype(q.dtype)

    out = lax.map(one_block, (qs, gs, jnp.arange(nq, dtype=jnp.int32)))
    return out.transpose(1, 0, 2, 3, 4).reshape(B, Tq, H, hd)


def retention(q, k, v, S0):
    B, T, H, dk = q.shape
    C = RET_CHUNK if T % RET_CHUNK == 0 else T
    n = T // C
    lg = jnp.log1p(-jnp.exp2(-5.0 - jnp.arange(H, dtype=jnp.float32)))
    i = jnp.arange(C, dtype=jnp.float32)
    diff = i[:, None] - i[None, :]
    dmask = jnp.where(diff >= 0, jnp.exp(jnp.maximum(diff, 0.0)[None] * lg[:, None, None]), 0.0)
    cross = jnp.exp((i + 1.0)[:, None] * lg[None, :])
    kdec = jnp.exp((C - 1.0 - i)[:, None] * lg[None, :])
    cdec = jnp.exp(C * lg)

    def to_chunks(a):
        return a.astype(jnp.float32).reshape(B, n, C, H, a.shape[-1]).transpose(1, 0, 2, 3, 4)

    qs, ks, vs = to_chunks(q), to_chunks(k * dk ** -0.5), to_chunks(v)

    def step(S, xs):
        qc, kc, vc = xs
        att = jnp.einsum('bihd,bjhd->bhij', qc, kc) * dmask
        o = jnp.einsum('bhij,bjhe->bihe', att, vc) + jnp.einsum('bihd,bhde->bihe', qc, S) * cross[None, :, :, None]
        S = S * cdec[None, :, None, None] + jnp.einsum('bjhd,bjhe->bhde', kc * kdec[None, :, :, None], vc)
        return S, o

    S, o = lax.scan(step, S0.astype(jnp.float32), (qs, ks, vs))
    return o.transpose(1, 0, 2, 3, 4).reshape(B, T, H, -1), S


def fox_attend(q, rows, F, offset):
    B, Tq, H, hd = q.shape
    Tk = rows.shape[1]
    G, R = FOX_KV_HEADS, FOX_GROUP
    qb = Q_BLOCK if Tq % Q_BLOCK == 0 else Tq
    nq = Tq // qb
    qs = (q * hd ** -0.5).reshape(B, nq, qb, G, R, hd).transpose(1, 0, 2, 3, 4, 5)
    Fq = F[:, offset:].reshape(B, nq, qb, G, R).transpose(1, 0, 2, 3, 4)
    Fk = F.reshape(B, Tk, G, R).transpose(0, 2, 3, 1)
    k_pos = jnp.arange(Tk)

    def one_block(xs):
        qg, fq, i = xs
        t = offset + i * qb + jnp.arange(qb)
        s = jnp.einsum('bqgrd,bkgd->bgrqk', qg, rows[:, :, 0]).astype(jnp.float32)
        s = s + fq.transpose(0, 2, 3, 1)[..., None] - Fk[:, :, :, None, :]
        p = masked_softmax(s, k_pos[None, :] <= t[:, None])
        o = jnp.einsum('bgrqk,bkgd->bqgrd', p, rows[:, :, 1])
        return o.reshape(B, qb, H, hd).astype(q.dtype)

    out = lax.map(one_block, (qs, Fq, jnp.arange(nq, dtype=jnp.int32)))
    return out.transpose(1, 0, 2, 3, 4).reshape(B, Tq, H, hd)


def peer_ffn(xn, wq, k1, k2, u, v):
    B, T, D = xn.shape
    n = B * T
    c = PEER_CHUNK
    pad = (-n) % c
    xt = jnp.pad(xn.reshape(n, D), ((0, pad), (0, 0))).reshape(-1, c, D)
    half = PEER_DK // 2

    def one_chunk(xc):
        q = (xc @ wq).reshape(c, PEER_HEADS, PEER_DK)
        s1 = jnp.einsum('chd,nd->chn', q[..., :half], k1).astype(jnp.float32)
        s2 = jnp.einsum('chd,nd->chn', q[..., half:], k2).astype(jnp.float32)
        v1, i1 = lax.top_k(s1, PEER_TOPK)
        v2, i2 = lax.top_k(s2, PEER_TOPK)
        cand = (v1[..., :, None] + v2[..., None, :]).reshape(c, PEER_HEADS, PEER_TOPK * PEER_TOPK)
        vals, ci = lax.top_k(cand, PEER_TOPK)
        e = (jnp.take_along_axis(i1, ci // PEER_TOPK, axis=-1) * N_KEYS
             + jnp.take_along_axis(i2, ci % PEER_TOPK, axis=-1))
        g = jax.nn.softmax(vals, axis=-1)
        act = jax.nn.gelu(jnp.einsum('cd,chkd->chk', xc, u[e]).astype(jnp.float32))
        return jnp.einsum('chk,chkd->cd', g * act, v[e]).astype(xc.dtype)

    out = lax.map(one_chunk, xt).reshape(-1, D)[:n]
    return out.reshape(B, T, D)


def even_mixer(xn, w_in, w_out, pe_k, w1_k, w2_k, pe_v, w1_v, w2_v, gn_w, kv_past, win_buf, S0):
    B, T, _ = xn.shape
    hd, G = HEAD_DIM, NSA_KV_HEADS
    q, kc, vc, ks, vs, kw, vw, gt, rq, rk, rv, rg = split_cols(xn @ w_in, ab_sizes())
    q = q.reshape(B, T, NSA_HEADS, hd)
    new_rows = jnp.stack([a.reshape(B, T, G, hd) for a in (kc, vc, ks, vs)], axis=2)
    new_win = jnp.stack([kw.reshape(B, T, G, hd), vw.reshape(B, T, G, hd)], axis=2)
    pad = jnp.zeros((B, WINDOW, 2, G, hd), new_win.dtype)
    if kv_past is None:
        offset, wstart = 0, 0
        rows = new_rows
        win_arr = jnp.concatenate([pad, new_win], axis=1)
        win_state = new_win[:, -min(WINDOW, T):]
        S0 = jnp.zeros((B, RET_HEADS, RET_DK, RET_DV), jnp.float32)
    else:
        offset = kv_past.shape[1]
        wb = win_buf.shape[1]
        wstart = offset - wb
        rows = jnp.concatenate([kv_past.astype(new_rows.dtype), new_rows], axis=1)
        buf_all = jnp.concatenate([win_buf.astype(new_win.dtype), new_win], axis=1)
        win_arr = jnp.concatenate([pad, buf_all], axis=1)
        win_state = buf_all[:, -wb:]
    o_a = nsa_attend(q, gt.reshape(B, T, NSA_HEADS, 3), rows, win_arr, offset, wstart,
                     pe_k, w1_k, w2_k, pe_v, w1_v, w2_v)
    o_b, S = retention(rq.reshape(B, T, RET_HEADS, RET_DK), rk.reshape(B, T, RET_HEADS, RET_DK),
                       rv.reshape(B, T, RET_HEADS, RET_DV), S0)
    mu = jnp.mean(o_b, axis=-1, keepdims=True)
    var = jnp.mean(jnp.square(o_b - mu), axis=-1, keepdims=True)
    o_b = ((o_b - mu) * lax.rsqrt(var + GN_EPS)).reshape(B, T, RET_HEADS * RET_DV) * gn_w.astype(jnp.float32)
    o_b = (jax.nn.silu(rg.astype(jnp.float32)) * o_b).astype(xn.dtype)
    out = jnp.concatenate([o_a.reshape(B, T, NSA_HEADS * hd), o_b], axis=-1) @ w_out
    return out, new_rows, win_state, S


def odd_mixer(xn, w_in, b_f, w_out, kv_past, logf_past):
    B, T, _ = xn.shape
    hd, H, G = HEAD_DIM, FOX_HEADS, FOX_KV_HEADS
    q, k, v, fl = split_cols(xn @ w_in, c_sizes())
    new_rows = jnp.stack([k.reshape(B, T, G, hd), v.reshape(B, T, G, hd)], axis=2)
    new_logf = jax.nn.log_sigmoid(fl.astype(jnp.float32) + b_f.astype(jnp.float32))
    if kv_past is None:
        offset = 0
        rows, logf_all = new_rows, new_logf
    else:
        offset = kv_past.shape[1]
        rows = jnp.concatenate([kv_past.astype(new_rows.dtype), new_rows], axis=1)
        logf_all = jnp.concatenate([logf_past.astype(jnp.float32), new_logf], axis=1)
    F = jnp.cumsum(logf_all, axis=1)
    o = fox_attend(q.reshape(B, T, H, hd), rows, F, offset)
    return o.reshape(B, T, H * hd) @ w_out, new_rows, new_logf


def setup_inputs(seed: int = 0) -> dict:
    key = jax.random.key(seed)
    ks = iter(jax.random.split(key, 32))

    def nrm(shape, scale):
        return jax.random.normal(next(ks), shape, jnp.float32) * scale

    n_pages = PAST_LEN // PAGE_SIZE
    n_used = DEC_BATCH * n_pages
    n_pool = n_used + n_used // 4
    wb = min(WINDOW, PAST_LEN)
    G, Gf, hd = NSA_KV_HEADS, FOX_KV_HEADS, HEAD_DIM
    ab_cols = sum(ab_sizes())
    c_cols = sum(c_sizes())
    mix_w = NSA_HEADS * hd + RET_HEADS * RET_DV
    return {
        'x_prompt': nrm((BATCH, SEQ, D_MODEL), 1.0),
        'x_sample': nrm((DEC_BATCH, DEC_SEQ, D_MODEL), 1.0),
        'cache_nsa_kv': nrm((N_EVEN, n_pool, PAGE_SIZE, 4, G, hd), 1.0),
        'cache_nsa_win': nrm((N_EVEN, DEC_BATCH, wb, 2, G, hd), 1.0),
        'state_ret': nrm((N_EVEN, DEC_BATCH, RET_HEADS, RET_DK, RET_DV), 1.0),
        'cache_fox_kv': nrm((N_ODD, n_pool, PAGE_SIZE, 2, Gf, hd), 1.0),
        'cache_fox_logf': jax.nn.log_sigmoid(nrm((N_ODD, n_pool, PAGE_SIZE, FOX_HEADS), 0.5) + 3.0),
        'page_table': jax.random.permutation(next(ks), n_pool)[:n_used].reshape(DEC_BATCH, n_pages).astype(jnp.int32),
        'norm_mix': 1.0 + nrm((DEPTH, D_MODEL), 0.05),
        'norm_ffn': 1.0 + nrm((DEPTH, D_MODEL), 0.05),
        'norm_final': 1.0 + nrm((D_MODEL,), 0.05),
        'w_in_ab': nrm((N_EVEN, D_MODEL, ab_cols), D_MODEL ** -0.5),
        'w_out_ab': nrm((N_EVEN, mix_w, D_MODEL), mix_w ** -0.5),
        'cmp_pe_k': nrm((N_EVEN, CMP_BLOCK, hd), 0.1),
        'cmp_w1_k': nrm((N_EVEN, CMP_BLOCK * hd, CMP_HIDDEN), (CMP_BLOCK * hd) ** -0.5),
        'cmp_w2_k': nrm((N_EVEN, CMP_HIDDEN, hd), CMP_HIDDEN ** -0.5),
        'cmp_pe_v': nrm((N_EVEN, CMP_BLOCK, hd), 0.1),
        'cmp_w1_v': nrm((N_EVEN, CMP_BLOCK * hd, CMP_HIDDEN), (CMP_BLOCK * hd) ** -0.5),
        'cmp_w2_v': nrm((N_EVEN, CMP_HIDDEN, hd), CMP_HIDDEN ** -0.5),
        'ret_gn': 1.0 + nrm((N_EVEN, RET_HEADS * RET_DV), 0.05),
        'w_in_c': nrm((N_ODD, D_MODEL, c_cols), D_MODEL ** -0.5),
        'b_forget': 3.0 + nrm((N_ODD, FOX_HEADS), 0.5),
        'w_out_c': nrm((N_ODD, FOX_HEADS * hd, D_MODEL), (FOX_HEADS * hd) ** -0.5),
        'peer_wq': nrm((DEPTH, D_MODEL, PEER_HEADS * PEER_DK), D_MODEL ** -0.5),
        'peer_k1': nrm((DEPTH, N_KEYS, PEER_DK // 2), (PEER_DK // 2) ** -0.5),
        'peer_k2': nrm((DEPTH, N_KEYS, PEER_DK // 2), (PEER_DK // 2) ** -0.5),
        'peer_u': nrm((DEPTH, N_EXPERTS, D_MODEL), D_MODEL ** -0.5),
        'peer_v': nrm((DEPTH, N_EXPERTS, D_MODEL), (PEER_HEADS * PEER_TOPK) ** -0.5),
    }


def reference(x_prompt, x_sample, cache_nsa_kv, cache_nsa_win, state_ret, cache_fox_kv, cache_fox_logf,
              page_table, norm_mix, norm_ffn, norm_final, w_in_ab, w_out_ab, cmp_pe_k, cmp_w1_k, cmp_w2_k,
              cmp_pe_v, cmp_w1_v, cmp_w2_v, ret_gn, w_in_c, b_forget, w_out_c, peer_wq, peer_k1, peer_k2,
              peer_u, peer_v):
    n_pages = page_table.shape[1]

    def gather_pages(pool):
        g = pool[page_table]
        return g.reshape((g.shape[0], n_pages * g.shape[2]) + g.shape[3:])

    xp, xs = x_prompt, x_sample
    nsa_p, win_p, ret_p, nsa_s, win_s, ret_s = [], [], [], [], [], []
    fox_p, logf_p, fox_s, logf_s = [], [], [], []
    for l in range(DEPTH):
        hp = rmsnorm(xp, norm_mix[l])
        hs = rmsnorm(xs, norm_mix[l])
        if l % 2 == 0:
            e = l // 2
            w = (w_in_ab[e], w_out_ab[e], cmp_pe_k[e], cmp_w1_k[e], cmp_w2_k[e],
                 cmp_pe_v[e], cmp_w1_v[e], cmp_w2_v[e], ret_gn[e])
            op, r_p, b_p, st_p = even_mixer(hp, *w, None, None, None)
            os_, r_s, b_s, st_s = even_mixer(hs, *w, gather_pages(cache_nsa_kv[e]), cache_nsa_win[e], state_ret[e])
            nsa_p.append(r_p)
            win_p.append(b_p)
            ret_p.append(st_p)
            nsa_s.append(r_s)
            win_s.append(b_s)
            ret_s.append(st_s)
        else:
            o = l // 2
            op, r_p, lf_p = odd_mixer(hp, w_in_c[o], b_forget[o], w_out_c[o], None, None)
            os_, r_s, lf_s = odd_mixer(hs, w_in_c[o], b_forget[o], w_out_c[o],
                                       gather_pages(cache_fox_kv[o]), gather_pages(cache_fox_logf[o]))
            fox_p.append(r_p)
            logf_p.append(lf_p)
            fox_s.append(r_s)
            logf_s.append(lf_s)
        xp = xp + op
        xs = xs + os_
        xp = xp + peer_ffn(rmsnorm(xp, norm_ffn[l]), peer_wq[l], peer_k1[l], peer_k2[l], peer_u[l], peer_v[l])
        xs = xs + peer_ffn(rmsnorm(xs, norm_ffn[l]), peer_wq[l], peer_k1[l], peer_k2[l], peer_u[l], peer_v[l])
    y_prompt = rmsnorm(xp, norm_final)
    y_sample = rmsnorm(xs, norm_final)
    p_nsa_kv = jnp.stack(nsa_p)
    p_nsa_win = jnp.stack(win_p)
    p_ret = jnp.stack(ret_p)
    p_fox_kv = jnp.stack(fox_p)
    p_fox_logf = jnp.stack(logf_p)
    s_nsa_kv = jnp.stack(nsa_s)
    s_nsa_win = jnp.stack(win_s)
    s_ret = jnp.stack(ret_s)
    s_fox_kv = jnp.stack(fox_s)
    s_fox_logf = jnp.stack(logf_s)
    return (y_prompt, y_sample, p_nsa_kv, p_nsa_win, p_ret, p_fox_kv, p_fox_logf,
            s_nsa_kv, s_nsa_win, s_ret, s_fox_kv, s_fox_logf)
```

```python
import math
from contextlib import ExitStack

import numpy as np
import concourse.bass as bass
import concourse.mybir as mybir
from concourse.bass_utils import run_bass_kernel_spmd

F32 = mybir.dt.float32
BF16 = mybir.dt.bfloat16
I32 = mybir.dt.int32
AF = mybir.ActivationFunctionType
ALU = mybir.AluOpType
AX = mybir.AxisListType

D = 2048
KC = D // 128
SEQ = 4096
NT = SEQ // 128
NS = 32
EPS = 1e-6
AB_COLS = 6680
C_KV0, C_KV1 = 1024, 2560
C_RK0, C_RV1 = 3608, 5656

N_DMA_SEMS = 24
SEM_ROLL = 30000


class Res:
    __slots__ = ("name", "w", "r")

    def __init__(self, name):
        self.name = name
        self.w = {}
        self.r = []


class Sched:
    ENGS = ("pe", "act", "dve", "pool", "sp")

    def __init__(self, nc, stack):
        self.nc = nc
        self.stack = stack
        self.streams = {e: [] for e in self.ENGS}
        self.sems = {}
        self.cur = {}
        self.gen = {e: 0 for e in self.ENGS}
        for e in self.ENGS:
            self._new_eng_sem(e)
        self.dma_sems = []
        for i in range(N_DMA_SEMS):
            k = "dma%d" % i
            self.sems[k] = stack.enter_context(nc.semaphore(k))
            self.dma_sems.append([k, 0])
        self.dma_rr = 0
        self.seen = {e: {} for e in self.ENGS}
        self.n_ops = 0

    def _new_eng_sem(self, e):
        k = "%s_g%d" % (e, self.gen[e])
        self.gen[e] += 1
        self.sems[k] = self.stack.enter_context(self.nc.semaphore(k))
        self.cur[e] = [k, 0]

    def _need(self, eng, tok, waits):
        if tok is None:
            return
        k, v = tok
        if self.seen[eng].get(k, 0) >= v:
            return
        self.seen[eng][k] = v
        waits[k] = max(waits.get(k, 0), v)

    def op(self, eng, fn, reads=(), writes=(), pe_chain=False):
        waits = {}
        for r in reads:
            for kv in r.w.items():
                self._need(eng, kv, waits)
        for w in writes:
            for kv in w.w.items():
                if not (pe_chain and kv[0].startswith("pe_")):
                    self._need(eng, kv, waits)
            for t in w.r:
                self._need(eng, t, waits)
        cur = self.cur[eng]
        if cur[1] >= SEM_ROLL:
            self._new_eng_sem(eng)
            cur = self.cur[eng]
        cur[1] += 1
        tok = (cur[0], cur[1])
        for r in reads:
            r.r.append(tok)
        for w in writes:
            w.w[tok[0]] = tok[1]
            w.r = []
        self.streams[eng].append((list(waits.items()), fn, (cur[0], 1)))
        self.n_ops += 1
        return tok

    def dma(self, eng, fn, reads=(), writes=()):
        waits = {}
        for r in reads:
            for kv in r.w.items():
                self._need(eng, kv, waits)
        for w in writes:
            for kv in w.w.items():
                self._need(eng, kv, waits)
            for t in w.r:
                self._need(eng, t, waits)
        ds = self.dma_sems[self.dma_rr]
        self.dma_rr = (self.dma_rr + 1) % len(self.dma_sems)
        if ds[1] > 0:
            self._need(eng, (ds[0], ds[1]), waits)
        if ds[1] >= SEM_ROLL:
            k = ds[0] + "x"
            self.sems[k] = self.stack.enter_context(self.nc.semaphore(k))
            ds[0], ds[1] = k, 0
        ds[1] += 16
        tok = (ds[0], ds[1])
        for r in reads:
            r.r.append(tok)
        for w in writes:
            w.w[tok[0]] = tok[1]
            w.r = []
        self.streams[eng].append((list(waits.items()), fn, (ds[0], 16)))
        self.n_ops += 1
        return tok

    def barrier(self):
        toks = [(c[0], c[1]) for c in self.cur.values() if c[1] > 0]
        toks += [(d[0], d[1]) for d in self.dma_sems if d[1] > 0]
        for e in self.ENGS:
            waits = {}
            for t in toks:
                self._need(e, t, waits)
            if waits:
                self.streams[e].append((list(waits.items()), None, None))

    def final_wait(self, eng, resources):
        waits = {}
        for r in resources:
            for kv in r.w.items():
                self._need(eng, kv, waits)
        self.streams[eng].append((list(waits.items()), None, None))

    def emit(self):
        nc = self.nc
        sems = self.sems
        streams = self.streams

        def run(engobj, lst):
            for waits, fn, inc in lst:
                for k, v in waits:
                    engobj.wait_ge(sems[k], v)
                if fn is not None:
                    ins = fn(engobj)
                    ins.then_inc(sems[inc[0]], inc[1])

        self.streams = {e: [] for e in self.ENGS}
        with nc.Block() as block:
            @block.tensor
            def _(e):
                run(e, streams["pe"])

            @block.scalar
            def _(e):
                run(e, streams["act"])

            @block.vector
            def _(e):
                run(e, streams["dve"])

            @block.gpsimd
            def _(e):
                run(e, streams["pool"])

            @block.sync
            def _(e):
                run(e, streams["sp"])


class T:
    def __init__(self, t, name):
        self.t = t
        self.r = Res(name)

    def __getitem__(self, k):
        return self.t[k]


TOK = SEQ + NS
COLS = dict(q=0, kc=1024, vc=1280, ks=1536, vs=1792, kw=2048, vw=2304, gt=2560,
            rq=2584, rk=3608, rv=4632, rg=5656)
FMC = ([(COLS["q"] + 128 * h, 128) for h in range(8)] +
       [(COLS["kc"] + 128 * g, 128) for g in range(2)] +
       [(COLS["vc"] + 128 * g, 128) for g in range(2)] +
       [(COLS["ks"] + 128 * g, 128) for g in range(2)] +
       [(COLS["kw"] + 128 * g, 128) for g in range(2)] +
       [(COLS["rq"] + 128 * h, 128) for h in range(8)] +
       [(COLS["rk"] + 128 * h, 128) for h in range(8)] +
       [(COLS["gt"], 24)])
FM_Q, FM_KC, FM_VC, FM_KS, FM_KW, FM_RQ, FM_RK, FM_GT = 0, 8, 10, 12, 14, 16, 24, 32
NFM = len(FMC)


def ret_consts():
    h = np.arange(8, dtype=np.float64)
    return np.log1p(-np.exp2(-5.0 - h))


def host_tables():
    f32 = np.float32
    lg = ret_consts()
    t = {}
    j = np.arange(128, dtype=np.float64)
    t["kd_p"] = (np.exp((127.0 - j)[:, None] * lg[None, :]) * 128.0 ** -0.5).astype(f32)
    t["cd_p"] = np.broadcast_to(np.exp(128.0 * lg)[None, :], (128, 8)).astype(f32).copy()
    js = np.arange(8, dtype=np.float64)
    kdec_s = np.exp((7.0 - js)[:, None] * lg[None, :]) * 128.0 ** -0.5
    kd_s = np.zeros((NS, 4, 8), f32)
    for b in range(4):
        kd_s[8 * b:8 * b + 8, b, :] = kdec_s
    t["kd_s"] = kd_s.reshape(NS, 32)
    t["cd_s"] = np.broadcast_to(np.exp(8.0 * lg)[None, :], (128, 8)).astype(f32).copy()
    ii = j[None, :] - j[:, None]
    dm = np.where(ii[:, None, :] >= 0, np.exp(np.maximum(ii, 0)[:, None, :] * lg[None, :, None]), 0.0)
    t["dmT_p"] = (dm * 128.0 ** -0.5).astype(f32).reshape(128, 1024)
    cr = np.exp((j + 1.0)[None, :] * lg[:, None])
    t["cross_p"] = np.broadcast_to(cr.reshape(1, 1024), (128, 1024)).astype(f32).copy()
    r = np.arange(NS)
    same = (r[:, None] // 8) == (r[None, :] // 8)
    dd = (r[None, :] % 8) - (r[:, None] % 8)
    dms = np.where((same & (dd >= 0))[:, None, :],
                   np.exp(np.maximum(dd, 0)[:, None, :] * lg[None, :, None]), 0.0) * 128.0 ** -0.5
    t["dmT_s"] = dms.astype(f32).reshape(NS, 8 * NS)
    crs = np.exp(((r % 8) + 1.0)[None, :] * lg[:, None])
    cs = np.zeros((4, 8, NS), np.float64)
    for b in range(4):
        cs[b][:, 8 * b:8 * b + 8] = crs[:, 8 * b:8 * b + 8]
    t["cross_s"] = np.broadcast_to(cs.reshape(1, 4 * 8 * NS), (128, 4 * 8 * NS)).astype(f32).copy()
    NEGM = -30000.0
    slopes = np.exp2(-8.0 * (np.arange(8) + 1.0) / 8.0)
    n64 = np.arange(64)
    i32 = np.arange(32)
    q128 = np.arange(128)
    valid = (64 * n64[:, None, None] + 63) <= (128 * i32[None, :, None] + q128[None, None, :])
    t["cmaskT"] = np.where(valid, 0.0, NEGM).astype(f32).reshape(64, 32 * 128)
    cb = slopes[None, None, :] * (64.0 * n64[:, None, None] + 63.0 - 128.0 * i32[None, :, None])
    cb = np.where((n64[:, None, None] <= 2 * i32[None, :, None] + 1), cb, NEGM)
    t["cbias"] = cb.astype(f32).reshape(64, 32 * 8)
    dl = np.arange(65)
    t["ab"] = (slopes[None, None, :] * (q128[:, None, None] - 128.0 * dl[None, :, None])).astype(f32).reshape(128, 65 * 8)
    t["tri_lo"] = np.where(q128[:, None] <= q128[None, :], 0.0, NEGM).astype(f32)
    t["tri_hi"] = np.where(q128[:, None] >= q128[None, :], 0.0, NEGM).astype(f32)
    n128 = np.arange(128)
    j64 = np.arange(64)
    es = (n128[:, None, None] == (2 * j64[None, :, None] + (q128[None, None, :] >= 64)))
    t["esel"] = es.astype(f32).reshape(128, 64 * 128)
    cur = 2 * i32[None, :, None] + (q128[:, None, None] >= 64)
    nn = n64[None, None, :]
    forced = (nn == 0) | (nn == cur) | (nn == cur - 1)
    invalid = nn > cur
    t["scA"] = np.where(forced | invalid, 0.0, 1.0).astype(f32).reshape(128, 32 * 64)
    t["scB"] = np.where(forced, 5.0, np.where(invalid, -1.0, 0.0)).astype(f32).reshape(128, 32 * 64)
    t["cbias_s"] = (slopes[None, :] * (64.0 * n128[:, None] + 63.0 - 8192.0)).astype(f32)
    a_s = np.ones((8, 128), f32); a_s[:, 0] = 0; a_s[:, 127] = 0
    b_s = np.zeros((8, 128), f32); b_s[:, 0] = 5; b_s[:, 127] = 5
    t["scA_s"] = a_s
    t["scB_s"] = b_s
    sr = np.zeros((24, 24, 128), f32)
    for r_ in range(24):
        sr[r_, r_, :] = 1.0
    t["selrow"] = sr.reshape(24, 24 * 128)
    return t


def own_blocks(c):
    jp = c // 2
    out = []
    for m in range(4):
        out += [8 * m + jp, 8 * m + 7 - jp]
    return out


def core_tables(c):
    f32 = np.float32
    blks = own_blocks(c)
    p = np.arange(128)
    qidx = np.stack([i * 128 + p for i in blks], 1).astype(np.int32)
    oh = np.zeros((128, 8, NT), f32)
    bm = np.zeros((128, 8, NT), f32)
    dmask = np.zeros((128, 8, 4, 128), f32)
    tri = np.where(p[:, None] <= p[None, :], 0.0, -30000.0)
    for s_, i in enumerate(blks):
        oh[:, s_, i] = 1.0
        bm[:, s_, i + 1:] = -30000.0
        jd0 = 8 * (s_ // 2) + 4 * (s_ % 2)
        for cnd in range(4):
            j = jd0 + cnd
            if j == i:
                dmask[:, s_, cnd, :] = tri
            elif j > i:
                dmask[:, s_, cnd, :] = -30000.0
    return dict(qidx=qidx, oh=oh, bm=bm, dmask=dmask.reshape(128, 8 * 4 * 128))


TABLE_SHAPES = dict(cmaskT=[64, 32 * 128], cbias=[64, 256], ab=[128, 65 * 8], tri_lo=[128, 128],
                    tri_hi=[128, 128], esel=[128, 64 * 128], scA=[128, 2048], scB=[128, 2048],
                    selrow=[24, 24 * 128], cbias_s=[128, 8], scA_s=[8, 128], scB_s=[8, 128],
                    kd_p=[128, 8], cd_p=[128, 8], kd_s=[NS, 32], cd_s=[128, 8],
                    dmT_p=[128, 1024], cross_p=[128, 1024], dmT_s=[NS, 8 * NS],
                    cross_s=[128, 4 * 8 * NS])


def build_nc(debug_outs=(), stage=99, NPOOL=2560, NEXP=16384, ksub=99):
    nc = bass.Bass("TRN2", target_bir_lowering=False)

    def din(name, shape, dt=F32):
        return nc.dram_tensor(name, list(shape), dt, kind="ExternalInput").ap()

    def dout(name, shape, dt=F32):
        return nc.dram_tensor(name, list(shape), dt, kind="ExternalOutput").ap()

    def dscr(name, shape, dt):
        if name in debug_outs:
            return nc.dram_tensor(name, list(shape), dt, kind="ExternalOutput").ap()
        return nc.dram_tensor(name, list(shape), dt).ap()

    xseq = din("xseq", [SEQ, D])
    xsamp = din("xsamp", [NS, D])
    nw_in = din("nw_mix0", [128, KC])
    w_in_ab = din("w_in_ab", [D, AB_COLS])
    win_cache = din("win_cache", [4, 512, 512])
    state_ret = din("state_ret", [4, 8, 128, 128])
    gn_in = din("ret_gn", [1, 1024])
    pt_in = din("page_table", [1, 256], I32)
    nsa_pool = din("nsa_pool", [NPOOL * 128, 1024])
    fox_pool = din("fox_pool", [NPOOL * 128, 1024])
    foxlf_pool = din("foxlf_pool", [NPOOL * 128, 16])
    nw_in1 = din("nw_mix1", [128, KC])
    nfin_in = din("norm_final", [1, D])
    w_in_c = din("w_in_c", [D, 3088])
    b_forget = din("b_forget", [1, 16])
    w_out_c = din("w_out_c", [D, D])
    qidx_in = din("qidx", [128, 8], I32)
    oh_in = din("oh", [128, 8, NT])
    bm_in = din("bm", [128, 8, NT])
    dmask_in = din("dmask", [128, 8 * 4 * 128])
    w_out_ab = din("w_out_ab", [D, D])
    nwf_in = din("nw_ffn", [2, 128, KC])
    peer_wq = din("peer_wq", [2, D, D])
    peer_k1 = din("peer_k1", [2, 128, 128]); peer_k2 = din("peer_k2", [2, 128, 128])
    peer_u = din("peer_u", [2, NEXP, D]); peer_v = din("peer_v", [2, NEXP, D])
    cmp_pe_k = din("cmp_pe_k", [64, 128]); cmp_pe_v = din("cmp_pe_v", [64, 128])
    cmp_w1_k = din("cmp_w1_k", [8192, 256]); cmp_w1_v = din("cmp_w1_v", [8192, 256])
    cmp_w2_k = din("cmp_w2_k", [256, 128]); cmp_w2_v = din("cmp_w2_v", [256, 128])
    tabs_in = {k: din("tb_" + k, v) for k, v in TABLE_SHAPES.items()}

    o_pkv = dout("o_pkv", [SEQ, 1024])
    o_pwin = dout("o_pwin", [512, 512])
    o_pret = dout("o_pret", [8, 128, 128])
    o_skv = dout("o_skv", [NS, 1024])
    o_swin = dout("o_swin", [4, 512, 512])
    o_sret = dout("o_sret", [4, 8, 128, 128])
    o_pfkv = dout("o_pfkv", [SEQ, 1024])
    o_plogf = dout("o_plogf", [SEQ, 16])
    o_sfkv = dout("o_sfkv", [NS, 1024])
    o_slogf = dout("o_slogf", [NS, 16])
    o_y = dout("o_y", [1024 + NS, D])

    XN0T = dscr("XN0T", [KC, 128, TOK], BF16)
    FM = dscr("FM", [NFM, 128, TOK], BF16)
    TMV = dscr("TMV", [TOK, 512], BF16)
    TMR = dscr("TMR", [TOK, 3072], BF16)
    ATTT = dscr("ATTT", [16, 128, TOK], BF16)
    r_XN0T, r_FM, r_TMV, r_TMR, r_ATTT = (Res(n) for n in ("XN0T", "FM", "TMV", "TMR", "ATTT"))

    with ExitStack() as st:
        S = Sched(nc, st)
        out_res = []
        uid = [0]

        def mk(stack, kind, name, shape, dt):
            uid[0] += 1
            nm = "%s%d_%s" % (kind, uid[0], name)
            f = nc.sbuf_tensor if kind == "sb" else nc.psum_tensor
            return T(stack.enter_context(f(nm, list(shape), dt)), nm)

        def MM(out, lhsT, rhs, start, stop, reads, writes):
            S.op("pe", lambda e: e.matmul(out=out, lhsT=lhsT, rhs=rhs, start=start, stop=stop),
                 reads=reads, writes=writes, pe_chain=True)

        def TR(out, in_, ident, reads, writes):
            S.op("pe", lambda e: e.transpose(out=out, in_=in_, identity=ident),
                 reads=reads, writes=writes, pe_chain=True)

        def ACT(out, in_, func, reads, writes, bias=None, scale=None, accum_out=None):
            kw = {}
            if bias is not None:
                kw["bias"] = bias
            if scale is not None:
                kw["scale"] = scale
            if accum_out is not None:
                kw["accum_out"] = accum_out
            S.op("act", lambda e: e.activation(out=out, in_=in_, func=func, **kw), reads=reads, writes=writes)

        def CP(eng, out, in_, reads, writes):
            if eng == "act":
                S.op("act", lambda e: e.copy(out=out, in_=in_), reads=reads, writes=writes)
            else:
                S.op(eng, lambda e: e.tensor_copy(out=out, in_=in_), reads=reads, writes=writes)

        def TT(eng, out, in0, in1, op, reads, writes):
            S.op(eng, lambda e: e.tensor_tensor(out=out, in0=in0, in1=in1, op=op), reads=reads, writes=writes)

        def TS(eng, out, in0, s1, s2, op0, op1, reads, writes):
            if op1 is None:
                S.op(eng, lambda e: e.tensor_scalar(out=out, in0=in0, scalar1=s1, scalar2=None, op0=op0),
                     reads=reads, writes=writes)
            else:
                S.op(eng, lambda e: e.tensor_scalar(out=out, in0=in0, scalar1=s1, scalar2=s2, op0=op0, op1=op1),
                     reads=reads, writes=writes)

        def MS(eng, out, val, writes):
            S.op(eng, lambda e: e.memset(out, val), writes=writes)

        def RED(out, in_, op, reads, writes):
            S.op("dve", lambda e: e.tensor_reduce(out=out, in_=in_, axis=AX.X, op=op), reads=reads, writes=writes)

        def RCP(out, in_, reads, writes):
            S.op("dve", lambda e: e.reciprocal(out=out, in_=in_), reads=reads, writes=writes)

        def DMA(eng, out, in_, reads, writes):
            S.dma(eng, lambda e: e.dma_start(out=out, in_=in_), reads=reads, writes=writes)

        def ODMA(eng, out, in_, reads):
            r = Res("out")
            out_res.append(r)
            S.dma(eng, lambda e: e.dma_start(out=out, in_=in_), reads=reads, writes=[r])

        def end_phase():
            S.barrier()
            S.emit()

        ident = mk(st, "sb", "ident", [128, 128], BF16)
        MS("pool", ident[:], 1.0, [ident.r])
        S.op("pool", lambda e: e.affine_select(out=ident[:], in_=ident[:], pattern=[[-1, 128]],
                                               compare_op=ALU.is_equal, fill=0.0, base=0,
                                               channel_multiplier=1),
             reads=[ident.r], writes=[ident.r])
        tb = {}
        BF_TABS = ("cmaskT", "tri_lo", "tri_hi", "esel", "scA", "scB", "selrow")
        NSA_TABS = ("cmaskT", "cbias", "ab", "tri_lo", "tri_hi", "esel", "scA", "scB", "selrow")
        LATE_TABS = NSA_TABS + ("cbias_s", "scA_s", "scB_s")

        def load_table(stack, k):
            shp = TABLE_SHAPES[k]
            if k in BF_TABS:
                tb[k] = mk(stack, "sb", "tb_" + k, shp, BF16)
                DMA("pool", tb[k][:], tabs_in[k], [], [tb[k].r])
            else:
                tb[k] = mk(stack, "sb", "tb_" + k, shp, F32)
                DMA("sp", tb[k][:], tabs_in[k], [], [tb[k].r])

        for k in TABLE_SHAPES:
            if k not in LATE_TABS:
                load_table(st, k)
        identF = mk(st, "sb", "identF", [128, 128], F32)
        CP("dve", identF[:], ident[:], [ident.r], [identF.r])
        ones_b = mk(st, "sb", "ones_b", [128, 128], BF16)
        MS("pool", ones_b[:], 1.0, [ones_b.r])

        def phase_norm(tiles, nw_dram, XNT, r_XNT, final_out=None):
            with ExitStack() as ph:
                if final_out is None:
                    nw = mk(ph, "sb", "nw", [128, KC], F32)
                    DMA("sp", nw[:], nw_dram, [], [nw.r])
                NB = 2
                xt = [mk(ph, "sb", "xt", [128, D], F32) for _ in range(NB)]
                junk = mk(ph, "sb", "junk", [128, D], BF16)
                ss = [mk(ph, "sb", "ss", [128, 2], F32) for _ in range(NB)]
                if final_out is None:
                    xs = [mk(ph, "sb", "xs", [128, D], BF16) for _ in range(NB)]
                    xnT = [mk(ph, "sb", "xnT", [128, KC, 128], BF16) for _ in range(NB)]
                    tp = [mk(ph, "ps", "tp", [128, 8, 128], BF16) for _ in range(2)]
                else:
                    nwb = mk(ph, "sb", "nwb", [128, D], F32)
                    DMA("sp", nwb[:], final_out[1].partition_broadcast(128), [], [nwb.r])
                    yo = [mk(ph, "sb", "yo", [128, D], F32) for _ in range(NB)]
                for ti, (src, P, rds, tok0) in enumerate(tiles):
                    i = ti % NB
                    X, SS = xt[i], ss[i]
                    DMA("sp", X[0:P, :], src, rds, [X.r])
                    MS("dve", SS[:], 0.0, [SS.r])
                    ACT(junk[0:P, :], X[0:P, :], AF.Square, [X.r], [junk.r, SS.r], accum_out=SS[0:P, 0:1])
                    TS("dve", SS[0:P, 1:2], SS[0:P, 0:1], 1.0 / D, EPS, ALU.mult, ALU.add, [SS.r], [SS.r])
                    S.op("act", (lambda o, i_: lambda e: e.sqrt(out=o, in_=i_))(SS[0:P, 1:2], SS[0:P, 1:2]),
                         reads=[SS.r], writes=[SS.r])
                    RCP(SS[0:P, 0:1], SS[0:P, 1:2], [SS.r], [SS.r])
                    if final_out is not None:
                        Y = yo[i]
                        TS("dve", Y[0:P, :], X[0:P, :], SS[0:P, 0:1], None, ALU.mult, None, [X.r, SS.r], [Y.r])
                        TT("dve", Y[0:P, :], Y[0:P, :], nwb[0:P, :], ALU.mult, [nwb.r], [Y.r])
                        ODMA("pool", final_out[0](tok0, P), Y[0:P, :], [Y.r])
                        continue
                    XS, XT = xs[i], xnT[i]
                    if P < 128:
                        MS("dve", XS[:], 0.0, [XS.r])
                    TS("dve", XS[0:P, :], X[0:P, :], SS[0:P, 0:1], None, ALU.mult, None, [X.r, SS.r], [XS.r])
                    for hlf in range(2):
                        for k in range(8):
                            kk = hlf * 8 + k
                            TR(tp[hlf][:, k, :], XS[:, kk * 128:(kk + 1) * 128], ident[:],
                               [XS.r, ident.r], [tp[hlf].r])
                        TT("dve", XT[:, hlf * 8:(hlf + 1) * 8, :], tp[hlf][:],
                           nw[:, hlf * 8:(hlf + 1) * 8].unsqueeze(2).to_broadcast([128, 8, 128]),
                           ALU.mult, [tp[hlf].r, nw.r], [XT.r])
                    DMA("pool", XNT[:, :, tok0:tok0 + P].rearrange("k p t -> p k t"),
                        XT[:, :, 0:P], [XT.r], [r_XNT])
                end_phase()

        TILES0 = [(xseq[t * 128:(t + 1) * 128, :], 128, [], t * 128) for t in range(NT)] + [(xsamp, NS, [], SEQ)]
        phase_norm(TILES0, nw_in, XN0T, r_XN0T)

        with ExitStack() as ph:
            w_view = w_in_ab.rearrange("(k p) c -> p k c", p=128)
            NW = 2
            Wg = [mk(ph, "sb", "Wg", [128, KC, 512], BF16) for _ in range(NW)]
            XTs = [mk(ph, "sb", "XTs", [128, KC, 512], BF16) for _ in range(2)]
            stg_f = [mk(ph, "sb", "stgf", [128, 512], F32) for _ in range(2)]
            stg_b = [mk(ph, "sb", "stgb", [128, 4, 512], BF16) for _ in range(2)]
            pp = [mk(ph, "ps", "pp", [128, 512], F32) for _ in range(4)]
            cnt = {"w": 0, "x": 0, "p": 0, "sf": 0, "sb": 0, "ev": 0}
            SUP = [(s * 512, 512) for s in range(8)] + [(SEQ, NS)]

            def load_w(cols):
                W = Wg[cnt["w"] % NW]
                cnt["w"] += 1
                c = 0
                for (c0, n) in cols:
                    DMA("pool", W[:, :, c:c + n], w_view[:, :, c0:c0 + n], [], [W.r])
                    c += n
                return W

            def load_x(t0, n):
                X = XTs[cnt["x"] % 2]
                cnt["x"] += 1
                DMA("sp", X[:, :, 0:n], XN0T[:, :, t0:t0 + n].rearrange("k p t -> p k t"), [r_XN0T], [X.r])
                return X

            def evac(out, in_, reads, writes):
                eng = "act" if cnt["ev"] % 2 == 0 else "dve"
                cnt["ev"] += 1
                CP(eng, out, in_, reads, writes)

            for g0 in range(0, NFM, 4):
                chunks = FMC[g0:g0 + 4]
                W = load_w(chunks)
                for (t0, n) in SUP:
                    X = load_x(t0, n)
                    SB_ = stg_b[cnt["sb"] % 2]
                    cnt["sb"] += 1
                    c = 0
                    for ci, (c0, ncol) in enumerate(chunks):
                        P_ = pp[cnt["p"] % 4]
                        cnt["p"] += 1
                        for k in range(KC):
                            MM(P_[0:ncol, 0:n], W[:, k, c:c + ncol], X[:, k, 0:n], k == 0, k == KC - 1,
                               [W.r, X.r], [P_.r])
                        evac(SB_[0:ncol, ci, 0:n], P_[0:ncol, 0:n], [P_.r], [SB_.r])
                        c += ncol
                    if len(chunks) == 4:
                        DMA("pool", FM[g0:g0 + 4, :, t0:t0 + n].rearrange("c p t -> p c t"),
                            SB_[:, :, 0:n], [SB_.r], [r_FM])
                    else:
                        DMA("pool", FM[g0, 0:24, t0:t0 + n], SB_[0:24, 0, 0:n], [SB_.r], [r_FM])

            TMG = [("kv", 1024), ("kv", 1536), ("kv", 2048),
                   ("r", 3608), ("r", 4120), ("r", 4632), ("r", 5144), ("r", 5656), ("r", 6168)]
            for gi, (kind, c0) in enumerate(TMG):
                W = load_w([(c0, 512)])
                for (t0, n) in SUP:
                    X = load_x(t0, n)
                    SB_ = stg_b[cnt["sb"] % 2]
                    cnt["sb"] += 1
                    nsub = (n + 127) // 128
                    for sub in range(nsub):
                        m = min(128, n - sub * 128)
                        P_ = pp[cnt["p"] % 4]
                        cnt["p"] += 1
                        for k in range(KC):
                            MM(P_[0:m, :], X[:, k, sub * 128:sub * 128 + m], W[:, k, :], k == 0, k == KC - 1,
                               [W.r, X.r], [P_.r])
                        tok = t0 + sub * 128
                        if kind == "kv":
                            SF = stg_f[cnt["sf"] % 2]
                            cnt["sf"] += 1
                            evac(SF[0:m, :], P_[0:m, :], [P_.r], [SF.r])
                            cc = c0 - 1024
                            if n == 512:
                                if cc < 1024:
                                    ODMA("pool", o_pkv[tok:tok + 128, cc:cc + 512], SF[:, :], [SF.r])
                                elif tok >= SEQ - 512:
                                    ODMA("pool", o_pwin[tok - (SEQ - 512):tok - (SEQ - 512) + 128, :], SF[:, :], [SF.r])
                            else:
                                if cc < 1024:
                                    ODMA("pool", o_skv[:, cc:cc + 512], SF[0:NS, :], [SF.r])
                                else:
                                    for b in range(4):
                                        ODMA("pool", o_swin[b, 504:512, :], SF[8 * b:8 * b + 8, :], [SF.r])
                                        ODMA("pool", o_swin[b, 0:504, :], win_cache[b, 8:512, :], [])
                            if cc >= 512:
                                CP("dve", SB_[0:m, sub, 0:256], SF[0:m, 256:512], [SF.r], [SB_.r])
                                vc0 = 0 if cc == 512 else 256
                                DMA("pool", TMV[tok:tok + m, vc0:vc0 + 256], SB_[0:m, sub, 0:256], [SB_.r], [r_TMV])
                        else:
                            evac(SB_[0:m, sub, :], P_[0:m, :], [P_.r], [SB_.r])
                            rc0 = c0 - 3608
                            DMA("pool", TMR[tok:tok + m, rc0:rc0 + 512], SB_[0:m, sub, :], [SB_.r], [r_TMR])
            end_phase()

        with ExitStack() as ph:
            gnw = mk(ph, "sb", "gnw", [128, 1024], F32)
            DMA("sp", gnw[:], gn_in.partition_broadcast(128), [], [gnw.r])
            Sst = mk(ph, "sb", "Sst", [128, 8, 128], F32)
            Sbf = mk(ph, "sb", "Sbf", [128, 8, 128], BF16)
            MS("dve", Sst[:], 0.0, [Sst.r])
            MS("dve", Sbf[:], 0.0, [Sbf.r])
            NB = 2
            QT = [mk(ph, "sb", "QT", [128, 8, 128], BF16) for _ in range(NB)]
            KT = [mk(ph, "sb", "KT", [128, 8, 128], BF16) for _ in range(NB)]
            RR = [mk(ph, "sb", "RR", [128, 3072], BF16) for _ in range(NB)]
            QsT = mk(ph, "sb", "QsT", [128, 8, 128], BF16)
            Kd = mk(ph, "sb", "Kd", [128, 1024], BF16)
            attm = mk(ph, "sb", "attm", [128, 8, 128], BF16)
            of = mk(ph, "sb", "of", [128, 8, 128], F32)
            sq = mk(ph, "sb", "sq", [128, 8, 128], F32)
            st8 = mk(ph, "sb", "st8", [128, 32], F32)
            sil = mk(ph, "sb", "sil", [128, 1024], F32)
            ob = mk(ph, "sb", "ob", [128, 1024], BF16)
            obT = mk(ph, "sb", "obT", [128, 8, 128], BF16)
            p_att = [mk(ph, "ps", "patt", [128, 4, 128], F32) for _ in range(2)]
            p_o = [mk(ph, "ps", "po", [128, 4, 128], F32) for _ in range(2)]
            p_s = [mk(ph, "ps", "pst", [128, 4, 128], F32) for _ in range(2)]
            p_t = mk(ph, "ps", "ptr", [128, 8, 128], BF16)

            def groupnorm_gate_store(P, RRt, tok0):
                RED(st8[0:P, 0:8], of[0:P], ALU.add, [of.r], [st8.r])
                TS("dve", st8[0:P, 0:8], st8[0:P, 0:8], 1.0 / 128, None, ALU.mult, None, [st8.r], [st8.r])
                TT("dve", of[0:P], of[0:P], st8[0:P, 0:8].unsqueeze(2).to_broadcast([P, 8, 128]),
                   ALU.subtract, [st8.r], [of.r])
                TT("dve", sq[0:P], of[0:P], of[0:P], ALU.mult, [of.r], [sq.r])
                RED(st8[0:P, 8:16], sq[0:P], ALU.add, [sq.r], [st8.r])
                TS("dve", st8[0:P, 8:16], st8[0:P, 8:16], 1.0 / 128, 1e-5, ALU.mult, ALU.add, [st8.r], [st8.r])
                S.op("act", (lambda o, i_: lambda e: e.sqrt(out=o, in_=i_))(st8[0:P, 8:16], st8[0:P, 8:16]),
                     reads=[st8.r], writes=[st8.r])
                RCP(st8[0:P, 16:24], st8[0:P, 8:16], [st8.r], [st8.r])
                TT("dve", of[0:P], of[0:P], st8[0:P, 16:24].unsqueeze(2).to_broadcast([P, 8, 128]),
                   ALU.mult, [st8.r], [of.r])
                TT("dve", of[0:P].rearrange("p h e -> p (h e)"), of[0:P].rearrange("p h e -> p (h e)"),
                   gnw[0:P, :], ALU.mult, [gnw.r], [of.r])
                ACT(sil[0:P, :], RRt[0:P, 2048:3072], AF.Silu, [RRt.r], [sil.r])
                if P < 128:
                    MS("dve", ob[:], 0.0, [ob.r])
                TT("dve", ob[0:P, :], of[0:P].rearrange("p h e -> p (h e)"), sil[0:P, :], ALU.mult,
                   [of.r, sil.r], [ob.r])
                for h in range(8):
                    TR(p_t[:, h, :], ob[:, h * 128:(h + 1) * 128], ident[:], [ob.r, ident.r], [p_t.r])
                CP("act", obT[:], p_t[:], [p_t.r], [obT.r])
                DMA("pool", ATTT[8:16, :, tok0:tok0 + P].rearrange("c p t -> p c t"), obT[:, :, 0:P],
                    [obT.r], [r_ATTT])

            for n in range(NT):
                i = n % NB
                t0 = n * 128
                Q, K, R_ = QT[i], KT[i], RR[i]
                DMA("sp", Q[:], FM[FM_RQ:FM_RQ + 8, :, t0:t0 + 128].rearrange("c p t -> p c t"), [r_FM], [Q.r])
                DMA("sp", K[:], FM[FM_RK:FM_RK + 8, :, t0:t0 + 128].rearrange("c p t -> p c t"), [r_FM], [K.r])
                DMA("sp", R_[:], TMR[t0:t0 + 128, :], [r_TMR], [R_.r])
                TT("dve", QsT[:].rearrange("p h i -> p (h i)"), Q[:].rearrange("p h i -> p (h i)"),
                   tb["cross_p"][:], ALU.mult, [Q.r, tb["cross_p"].r], [QsT.r])
                TT("dve", Kd[:].rearrange("p (h d) -> p h d", d=128), R_[:, 0:1024].rearrange("p (h d) -> p h d", d=128),
                   tb["kd_p"][:].unsqueeze(2).to_broadcast([128, 8, 128]), ALU.mult,
                   [R_.r, tb["kd_p"].r], [Kd.r])
                for hh in range(2):
                    for h4 in range(4):
                        h = hh * 4 + h4
                        MM(p_att[hh][:, h4, :], K[:, h, :], Q[:, h, :], True, True, [K.r, Q.r], [p_att[hh].r])
                    TT("dve", attm[:, hh * 4:(hh + 1) * 4, :], p_att[hh][:],
                       tb["dmT_p"][:, hh * 512:(hh + 1) * 512].rearrange("p (h i) -> p h i", i=128),
                       ALU.mult, [p_att[hh].r, tb["dmT_p"].r], [attm.r])
                for hh in range(2):
                    for h4 in range(4):
                        h = hh * 4 + h4
                        MM(p_o[hh][:, h4, :], attm[:, h, :], R_[:, 1024 + h * 128:1024 + (h + 1) * 128], True, False,
                           [attm.r, R_.r], [p_o[hh].r])
                        MM(p_o[hh][:, h4, :], QsT[:, h, :], Sbf[:, h, :], False, True,
                           [QsT.r, Sbf.r], [p_o[hh].r])
                    CP("act", of[:, hh * 4:(hh + 1) * 4, :], p_o[hh][:], [p_o[hh].r], [of.r])
                TT("dve", Sst[:], Sst[:], tb["cd_p"][:].unsqueeze(2).to_broadcast([128, 8, 128]), ALU.mult,
                   [tb["cd_p"].r], [Sst.r])
                for hh in range(2):
                    for h4 in range(4):
                        h = hh * 4 + h4
                        MM(p_s[hh][:, h4, :], Kd[:, h * 128:(h + 1) * 128],
                           R_[:, 1024 + h * 128:1024 + (h + 1) * 128], True, True, [Kd.r, R_.r], [p_s[hh].r])
                    TT("dve", Sst[:, hh * 4:(hh + 1) * 4, :], Sst[:, hh * 4:(hh + 1) * 4, :], p_s[hh][:], ALU.add,
                       [p_s[hh].r], [Sst.r])
                CP("act", Sbf[:], Sst[:], [Sst.r], [Sbf.r])
                groupnorm_gate_store(128, R_, t0)
            ODMA("pool", o_pret.rearrange("h d e -> d h e"), Sst[:], [Sst.r])

            S0 = mk(ph, "sb", "S0", [128, 4, 8, 128], F32)
            S0b = mk(ph, "sb", "S0b", [128, 4, 8, 128], BF16)
            Qm = mk(ph, "sb", "Qm", [128, 4, 8, NS], BF16)
            rkm = mk(ph, "sb", "rkm", [NS, 4, 1024], BF16)
            for b in range(4):
                DMA("sp", S0[:, b, :, :], state_ret[b].rearrange("h d e -> d h e"), [], [S0.r])
            CP("act", S0b[:].rearrange("p b h e -> p (b h e)"), S0[:].rearrange("p b h e -> p (b h e)"),
               [S0.r], [S0b.r])
            Q, K, R_ = QT[0], KT[0], RR[0]
            t0 = SEQ
            DMA("sp", Q[:, :, 0:NS], FM[FM_RQ:FM_RQ + 8, :, t0:t0 + NS].rearrange("c p t -> p c t"), [r_FM], [Q.r])
            DMA("sp", K[:, :, 0:NS], FM[FM_RK:FM_RK + 8, :, t0:t0 + NS].rearrange("c p t -> p c t"), [r_FM], [K.r])
            DMA("sp", R_[0:NS, :], TMR[t0:t0 + NS, :], [r_TMR], [R_.r])
            for b in range(4):
                TT("dve", Qm[:, b, :, :], Q[:, :, 0:NS],
                   tb["cross_s"][:, b * 8 * NS:(b + 1) * 8 * NS].rearrange("p (h i) -> p h i", i=NS),
                   ALU.mult, [Q.r, tb["cross_s"].r], [Qm.r])
                TT("dve", rkm[:, b, :].rearrange("p (h d) -> p h d", d=128),
                   R_[0:NS, 0:1024].rearrange("p (h d) -> p h d", d=128),
                   tb["kd_s"][:, b * 8:(b + 1) * 8].unsqueeze(2).to_broadcast([NS, 8, 128]), ALU.mult,
                   [R_.r, tb["kd_s"].r], [rkm.r])
            for hh in range(2):
                for h4 in range(4):
                    h = hh * 4 + h4
                    MM(p_att[hh][0:NS, h4, 0:NS], K[:, h, 0:NS], Q[:, h, 0:NS], True, True, [K.r, Q.r], [p_att[hh].r])
                TT("dve", attm[0:NS, hh * 4:(hh + 1) * 4, 0:NS], p_att[hh][0:NS, :, 0:NS],
                   tb["dmT_s"][:, hh * 4 * NS:(hh + 1) * 4 * NS].rearrange("p (h i) -> p h i", i=NS),
                   ALU.mult, [p_att[hh].r, tb["dmT_s"].r], [attm.r])
            for hh in range(2):
                for h4 in range(4):
                    h = hh * 4 + h4
                    MM(p_o[hh][0:NS, h4, :], attm[0:NS, h, 0:NS], R_[0:NS, 1024 + h * 128:1024 + (h + 1) * 128],
                       True, False, [attm.r, R_.r], [p_o[hh].r])
                    for b in range(4):
                        MM(p_o[hh][0:NS, h4, :], Qm[:, b, h, :], S0b[:, b, h, :], False, b == 3,
                           [Qm.r, S0b.r], [p_o[hh].r])
                CP("act", of[0:NS, hh * 4:(hh + 1) * 4, :], p_o[hh][0:NS], [p_o[hh].r], [of.r])
            for b in range(4):
                TT("dve", S0[:, b, :, :], S0[:, b, :, :], tb["cd_s"][:].unsqueeze(2).to_broadcast([128, 8, 128]),
                   ALU.mult, [tb["cd_s"].r], [S0.r])
                for hh in range(2):
                    for h4 in range(4):
                        h = hh * 4 + h4
                        MM(p_s[hh][:, h4, :], rkm[:, b, h * 128:(h + 1) * 128],
                           R_[0:NS, 1024 + h * 128:1024 + (h + 1) * 128], True, True, [rkm.r, R_.r], [p_s[hh].r])
                    TT("dve", S0[:, b, hh * 4:(hh + 1) * 4, :], S0[:, b, hh * 4:(hh + 1) * 4, :], p_s[hh][:],
                       ALU.add, [p_s[hh].r], [S0.r])
                ODMA("pool", o_sret[b].rearrange("h d e -> d h e"), S0[:, b, :, :], [S0.r])
            groupnorm_gate_store(NS, R_, SEQ)
            end_phase()

        SCALE = 128.0 ** -0.5
        with ExitStack() as ph:
            for k in NSA_TABS:
                load_table(ph, k)
            KcT = [mk(ph, "sb", "KcT", [128, 64], BF16) for _ in range(2)]
            Vc = [mk(ph, "sb", "Vc", [64, 128], BF16) for _ in range(2)]
            pmisc = mk(ph, "ps", "pmisc", [128, 512], F32)
            pmisc_b = mk(ph, "ps", "pmiscb", [128, 1024], BF16)
            with ExitStack() as phb:
                w1 = [mk(phb, "sb", "w1", [128, 64, 256], BF16) for _ in range(2)]
                w2 = [mk(phb, "sb", "w2", [128, 2, 128], BF16) for _ in range(2)]
                peT = [mk(phb, "sb", "peT", [128, 64], BF16) for _ in range(2)]
                pe_ld = mk(phb, "sb", "pe_ld", [64, 128], BF16)
                for kv, (w1d, w2d, ped) in enumerate(((cmp_w1_k, cmp_w2_k, cmp_pe_k), (cmp_w1_v, cmp_w2_v, cmp_pe_v))):
                    w1v = w1d.rearrange("(l d) h -> d l h", d=128)
                    for l0 in range(0, 64, 16):
                        DMA("pool", w1[kv][:, l0:l0 + 16, :], w1v[:, l0:l0 + 16, :], [], [w1[kv].r])
                    DMA("pool", w2[kv][:], w2d.rearrange("(hh h) e -> h hh e", h=128), [], [w2[kv].r])
                    DMA("pool", pe_ld[:], ped, [], [pe_ld.r])
                    TR(pmisc_b[:, 0:64], pe_ld[:], ident[0:64, 0:64], [pe_ld.r, ident.r], [pmisc_b.r])
                    CP("dve", peT[kv][:], pmisc_b[:, 0:64], [pmisc_b.r], [peT[kv].r])

                kcpe = mk(phb, "sb", "kcpe", [128, 64, 64], BF16)
                hidT = mk(phb, "sb", "hidT", [128, 2, 64], BF16)
                praw = mk(phb, "sb", "praw", [128, SEQ], BF16)

                def compress(kv, g, nb, src_T, dstT):
                    TT("dve", kcpe[:, 0:nb, :], src_T[:, 0:nb * 64].rearrange("p (n l) -> p n l", l=64),
                       peT[kv][:].unsqueeze(1).to_broadcast([128, nb, 64]), ALU.add,
                       [src_T.r, peT[kv].r], [kcpe.r])
                    for hh in range(2):
                        for l in range(64):
                            MM(pmisc[:, hh * 128:hh * 128 + nb], w1[kv][:, l, hh * 128:(hh + 1) * 128], kcpe[:, 0:nb, l],
                               l == 0, l == 63, [w1[kv].r, kcpe.r], [pmisc.r])
                    ACT(hidT[:, :, 0:nb], pmisc[:, 0:256].rearrange("p (a n) -> p a n", n=128)[:, :, 0:nb], AF.Gelu,
                        [pmisc.r], [hidT.r])
                    if kv == 0:
                        for hh in range(2):
                            MM(pmisc[:, 256:256 + nb], w2[kv][:, hh, :], hidT[:, hh, 0:nb], hh == 0, hh == 1,
                               [w2[kv].r, hidT.r], [pmisc.r])
                        CP("dve", dstT[:, 0:nb], pmisc[:, 256:256 + nb], [pmisc.r], [dstT.r])
                    else:
                        for hh in range(2):
                            MM(pmisc[0:nb, 384:512], hidT[:, hh, 0:nb], w2[kv][:, hh, :], hh == 0, hh == 1,
                               [w2[kv].r, hidT.r], [pmisc.r])
                        CP("dve", dstT[0:nb, :], pmisc[0:nb, 384:512], [pmisc.r], [dstT.r])

                for g in range(2):
                    DMA("sp", praw[:], FM[FM_KC + g, :, 0:SEQ], [r_FM], [praw.r])
                    compress(0, g, 64, praw, KcT[g])
                    DMA("sp", praw[:], FM[FM_VC + g, :, 0:SEQ], [r_FM], [praw.r])
                    compress(1, g, 64, praw, Vc[g])

                end_phase()

            KsT = mk(ph, "sb", "KsT", [128, 2, SEQ], BF16)
            KwT = mk(ph, "sb", "KwT", [128, 2, SEQ], BF16)
            Vs = mk(ph, "sb", "Vs", [128, NT, 512], BF16)
            for g in range(2):
                DMA("sp", KsT[:, g, :], FM[FM_KS + g, :, 0:SEQ], [r_FM], [KsT.r])
                DMA("sp", KwT[:, g, :], FM[FM_KW + g, :, 0:SEQ], [r_FM], [KwT.r])
            for j0 in range(0, NT, 8):
                DMA("sp", Vs[:, j0:j0 + 8, :], TMV[j0 * 128:(j0 + 8) * 128, :].rearrange("(j k) c -> k j c", k=128),
                    [r_TMV], [Vs.r])

            Qg = [mk(ph, "sb", "Qg", [128, 4, 128], BF16) for _ in range(2)]
            gts = mk(ph, "sb", "gts", [24, 128], BF16)
            sg = mk(ph, "sb", "sg", [24, 128], BF16)
            Pb = [mk(ph, "sb", "Pb", [128, 4, 128], BF16) for _ in range(3)]
            P32 = mk(ph, "sb", "P32", [64, 4, 128], F32)
            rc = mk(ph, "sb", "rc", [128, 512], F32)
            fac = mk(ph, "sb", "fac", [128, 512], F32)
            acc = mk(ph, "sb", "acc", [128, 512], F32)
            tmpo = mk(ph, "sb", "tmpo", [128, 512], F32)
            attb = mk(ph, "sb", "attb", [128, 4, 128], BF16)
            impT = mk(ph, "sb", "impT", [64, 128], F32)
            score = mk(ph, "sb", "score", [128, 64], F32)
            wk = mk(ph, "sb", "wk", [128, 64], F32)
            m16 = mk(ph, "sb", "m16", [128, 16], F32)
            negs = mk(ph, "sb", "negs", [128, 64], BF16)
            negsT = mk(ph, "sb", "negsT", [64, 128], BF16)
            pS = [mk(ph, "ps", "pS", [128, 512], F32) for _ in range(2)]
            pDen = mk(ph, "ps", "pDen", [128, 512], F32)
            pO = mk(ph, "ps", "pO", [128, 512], F32)
            pG = mk(ph, "ps", "pG", [128, 512], F32)
            cnt = {"s": 0, "p": 0}

            def bc4(ap2d, P, n):
                return ap2d.unsqueeze(1).to_broadcast([P, 4, n])

            def key_block(Q, nk, kT_ap, kT_res, v_ap, v_res, extra, bias_fn, first, last, keep32=False):
                ps = pS[cnt["s"] % 2]
                cnt["s"] += 1
                nmm = 1 + len(extra)
                MM(ps[0:nk, :], kT_ap, Q[:].rearrange("p h q -> p (h q)"), True, nmm == 1, [kT_res, Q.r], [ps.r])
                for ei, (l_ap, r_ap, rr) in enumerate(extra):
                    MM(ps[0:nk, :].rearrange("p (h q) -> p h q", q=128), l_ap, r_ap, False, ei == len(extra) - 1,
                       rr, [ps.r])
                P_ = Pb[cnt["p"] % 3]
                cnt["p"] += 1
                for h in range(4):
                    if keep32:
                        ACT(P32[0:nk, h, :], ps[0:nk, h * 128:(h + 1) * 128], AF.Exp, [ps.r] + bias_fn(h)[1], [P32.r],
                            bias=bias_fn(h)[0], scale=SCALE)
                    else:
                        ACT(P_[0:nk, h, :], ps[0:nk, h * 128:(h + 1) * 128], AF.Exp, [ps.r] + bias_fn(h)[1], [P_.r],
                            bias=bias_fn(h)[0], scale=SCALE)
                if keep32:
                    CP("dve", P_[0:nk], P32[0:nk], [P32.r], [P_.r])
                Pf = P_[0:nk].rearrange("p h q -> p (h q)")
                MM(pDen[:], ones_b[0:nk, :], Pf, first, last, [ones_b.r, P_.r], [pDen.r])
                MM(pO[:], v_ap, Pf, first, last, [v_res, P_.r], [pO.r])

            def finish_branch(br, g, first_branch):
                TS("dve", rc[:], pDen[:], 1e-30, None, ALU.max, None, [pDen.r], [rc.r])
                RCP(rc[:], rc[:], [rc.r], [rc.r])
                for h in range(4):
                    r_ = (4 * g + h) * 3 + br
                    MM(pG[:, h * 128:(h + 1) * 128], tb["selrow"][:, r_ * 128:(r_ + 1) * 128], sg[:], True, True,
                       [tb["selrow"].r, sg.r], [pG.r])
                TT("dve", fac[:], rc[:], pG[:], ALU.mult, [rc.r, pG.r], [fac.r])
                if first_branch:
                    TT("dve", acc[:], pO[:], fac[:], ALU.mult, [pO.r, fac.r], [acc.r])
                else:
                    TT("dve", tmpo[:], pO[:], fac[:], ALU.mult, [pO.r, fac.r], [tmpo.r])
                    TT("dve", acc[:], acc[:], tmpo[:], ALU.add, [tmpo.r], [acc.r])

            for i in range(NT):
                t0 = i * 128
                DMA("sp", gts[:], FM[FM_GT, 0:24, t0:t0 + 128], [r_FM], [gts.r])
                ACT(sg[:], gts[:], AF.Sigmoid, [gts.r], [sg.r])
                for g in range(2):
                    Q = Qg[g]
                    DMA("sp", Q[:], FM[FM_Q + 4 * g:FM_Q + 4 * g + 4, :, t0:t0 + 128].rearrange("c p t -> p c t"),
                        [r_FM], [Q.r])
                    key_block(Q, 64, KcT[g][:], KcT[g].r, Vc[g][:], Vc[g].r,
                              [(ident[0:64, 0:64], bc4(tb["cmaskT"][:, i * 128:(i + 1) * 128], 64, 128),
                                [ident.r, tb["cmaskT"].r])],
                              lambda h: (tb["cbias"][:, i * 8 + 4 * g + h:i * 8 + 4 * g + h + 1], [tb["cbias"].r]),
                              True, True, keep32=True)
                    finish_branch(0, g, True)
                    TT("dve", P32[:].rearrange("p h q -> p (h q)"), P32[:].rearrange("p h q -> p (h q)"),
                       rc[0:64, :], ALU.mult, [rc.r], [P32.r])
                    RED(impT[:], P32[:].rearrange("p h q -> p q h"), ALU.add, [P32.r], [impT.r])
                    TR(pG[:, 0:64], impT[:], identF[0:64, 0:64], [impT.r, identF.r], [pG.r])
                    TT("dve", score[:], pG[:, 0:64], tb["scA"][:, i * 64:(i + 1) * 64], ALU.mult,
                       [pG.r, tb["scA"].r], [score.r])
                    TT("dve", score[:], score[:], tb["scB"][:, i * 64:(i + 1) * 64], ALU.add,
                       [tb["scB"].r], [score.r])
                    S.op("dve", lambda e: e.max(out=m16[:, 0:8], in_=score[:]), reads=[score.r], writes=[m16.r])
                    S.op("dve", lambda e: e.match_replace(out=wk[:], in_to_replace=m16[:, 0:8], in_values=score[:],
                                                          imm_value=-1e30), reads=[score.r, m16.r], writes=[wk.r])
                    S.op("dve", lambda e: e.max(out=m16[:, 8:16], in_=wk[:]), reads=[wk.r], writes=[m16.r])
                    TS("dve", m16[:, 15:16], m16[:, 15:16], 0.0, None, ALU.max, None, [m16.r], [m16.r])
                    TS("dve", negs[:], score[:], m16[:, 15:16], -30000.0, ALU.is_lt, ALU.mult,
                       [score.r, m16.r], [negs.r])
                    TR(pmisc_b[0:64, 0:128], negs[:], ident[:], [negs.r, ident.r], [pmisc_b.r])
                    CP("dve", negsT[:], pmisc_b[0:64, 0:128], [pmisc_b.r], [negsT.r])
                    for j in range(i + 1):
                        extra = [(tb["esel"][0:64, j * 128:(j + 1) * 128], bc4(negsT[:], 64, 128),
                                  [tb["esel"].r, negsT.r])]
                        if j == i:
                            extra.append((ident[:], bc4(tb["tri_lo"][:], 128, 128), [ident.r, tb["tri_lo"].r]))
                        dlt = i - j
                        key_block(Q, 128, KsT[:, g, j * 128:(j + 1) * 128], KsT.r,
                                  Vs[:, j, g * 128:(g + 1) * 128], Vs.r, extra,
                                  lambda h, dlt=dlt: (tb["ab"][:, dlt * 8 + 4 * g + h:dlt * 8 + 4 * g + h + 1], [tb["ab"].r]),
                                  j == 0, j == i)
                    finish_branch(1, g, False)
                    j0 = max(0, i - 4)
                    for j in range(j0, i + 1):
                        extra = []
                        if j == i:
                            extra.append((ident[:], bc4(tb["tri_lo"][:], 128, 128), [ident.r, tb["tri_lo"].r]))
                        if j == i - 4:
                            extra.append((ident[:], bc4(tb["tri_hi"][:], 128, 128), [ident.r, tb["tri_hi"].r]))
                        dlt = i - j
                        key_block(Q, 128, KwT[:, g, j * 128:(j + 1) * 128], KwT.r,
                                  Vs[:, j, 256 + g * 128:256 + (g + 1) * 128], Vs.r, extra,
                                  lambda h, dlt=dlt: (tb["ab"][:, dlt * 8 + 4 * g + h:dlt * 8 + 4 * g + h + 1], [tb["ab"].r]),
                                  j == j0, j == i)
                    finish_branch(2, g, False)
                    CP("act", attb[:].rearrange("p h q -> p (h q)"), acc[:], [acc.r], [attb.r])
                    DMA("pool", ATTT[4 * g:4 * g + 4, :, t0:t0 + 128].rearrange("c p t -> p c t"), attb[:],
                        [attb.r], [r_ATTT])
            end_phase()

        def page_index_tile(stack):
            pti = mk(stack, "sb", "pti", [128, 256], I32)
            ptf = mk(stack, "sb", "ptf", [128, 256], F32)
            iof = mk(stack, "sb", "iof", [128, 1], F32)
            pidx = mk(stack, "sb", "pidx", [128, 256], I32)
            DMA("sp", pti[:], pt_in.partition_broadcast(128), [], [pti.r])
            S.op("pool", lambda e: e.iota(out=iof[:], pattern=[[0, 1]], base=0, channel_multiplier=1,
                                          allow_small_or_imprecise_dtypes=True), writes=[iof.r])
            CP("dve", ptf[:], pti[:], [pti.r], [ptf.r])
            TS("dve", ptf[:], ptf[:], 128.0, iof[:, 0:1], ALU.mult, ALU.add, [ptf.r, iof.r], [ptf.r])
            CP("dve", pidx[:], ptf[:], [ptf.r], [pidx.r])
            return pidx

        with ExitStack() as ph:
            for k in ("ab", "esel", "tri_lo", "tri_hi", "selrow", "cbias_s", "scA_s", "scB_s"):
                load_table(ph, k)
            pidx = page_index_tile(ph)
            if "PIDX" in debug_outs:
                o_pidx = nc.dram_tensor("PIDX", [128, 256], I32, kind="ExternalOutput").ap()
                ODMA("sp", o_pidx, pidx[:], [pidx.r])
            w1 = [mk(ph, "sb", "w1s", [128, 64, 256], BF16) for _ in range(2)]
            w2 = [mk(ph, "sb", "w2s", [128, 2, 128], BF16) for _ in range(2)]
            peT = [mk(ph, "sb", "peTs", [128, 64], BF16) for _ in range(2)]
            pe_ld = mk(ph, "sb", "pe_lds", [64, 128], BF16)
            pmisc = mk(ph, "ps", "pmiscs", [128, 512], F32)
            pmisc_b = mk(ph, "ps", "pmiscbs", [128, 1024], BF16)
            for kv, (w1d, w2d, ped) in enumerate(((cmp_w1_k, cmp_w2_k, cmp_pe_k), (cmp_w1_v, cmp_w2_v, cmp_pe_v))):
                w1v = w1d.rearrange("(l d) h -> d l h", d=128)
                for l0 in range(0, 64, 16):
                    DMA("pool", w1[kv][:, l0:l0 + 16, :], w1v[:, l0:l0 + 16, :], [], [w1[kv].r])
                DMA("pool", w2[kv][:], w2d.rearrange("(hh h) e -> h hh e", h=128), [], [w2[kv].r])
                DMA("pool", pe_ld[:], ped, [], [pe_ld.r])
                TR(pmisc_b[:, 0:64], pe_ld[:], ident[0:64, 0:64], [pe_ld.r, ident.r], [pmisc_b.r])
                CP("dve", peT[kv][:], pmisc_b[:, 0:64], [pmisc_b.r], [peT[kv].r])
            pgb = [mk(ph, "sb", "pgb", [128, 1024], F32) for _ in range(2)]
            kcT = mk(ph, "sb", "kcTs", [128, 8192], BF16)
            vcT = mk(ph, "sb", "vcTs", [128, 8192], BF16)
            ksT = mk(ph, "sb", "ksTs", [128, 8192], BF16)
            vsl = mk(ph, "sb", "vsl", [128, 64, 128], BF16)
            hidT = mk(ph, "sb", "hidTs", [128, 2, 128], BF16)
            KcTs = mk(ph, "sb", "KcTs", [128, 128], BF16)
            Vcs = mk(ph, "sb", "Vcs", [128, 128], BF16)
            wcb = [mk(ph, "sb", "wcb", [128, 512], F32) for _ in range(2)]
            kwT = mk(ph, "sb", "kwTs", [128, 4, 128], BF16)
            vwl = mk(ph, "sb", "vwl", [128, 4, 128], BF16)
            knT = mk(ph, "sb", "knT", [128, 2, 8], BF16)
            vnl = mk(ph, "sb", "vnl", [8, 512], BF16)
            Qs = mk(ph, "sb", "Qs", [128, 4, 8], BF16)
            gts = mk(ph, "sb", "gtss", [24, NS], BF16)
            sg = mk(ph, "sb", "sgs", [24, NS], BF16)
            Pb = [mk(ph, "sb", "Pbs", [128, 4, 8], BF16) for _ in range(3)]
            P32 = mk(ph, "sb", "P32s", [128, 4, 8], F32)
            rc = mk(ph, "sb", "rcs", [128, 32], F32)
            fac = mk(ph, "sb", "facs", [128, 32], F32)
            acc = mk(ph, "sb", "accs", [128, 32], F32)
            tmpo = mk(ph, "sb", "tmpos", [128, 32], F32)
            attall = mk(ph, "sb", "attall", [128, 2, 4, NS], BF16)
            MS("dve", attall[:], 0.0, [attall.r])
            impT = mk(ph, "sb", "impTs", [128, 8], F32)
            score = mk(ph, "sb", "scores", [8, 128], F32)
            wk = mk(ph, "sb", "wks", [8, 128], F32)
            m16 = mk(ph, "sb", "m16s", [8, 16], F32)
            negs = mk(ph, "sb", "negss", [8, 128], BF16)
            negsT = mk(ph, "sb", "negsTs", [128, 8], BF16)
            pT3 = [mk(ph, "ps", "pT3", [128, 8, 128], BF16) for _ in range(2)]
            gbb = [mk(ph, "sb", "gbb", [128, 4, 128], BF16) for _ in range(2)]
            wkb = mk(ph, "sb", "wkb", [128, 128], BF16)
            pS_f = [mk(ph, "ps", "pSs", [128, 512], F32) for _ in range(2)]
            pDen_f = mk(ph, "ps", "pDens", [128, 512], F32)
            pO_f = mk(ph, "ps", "pOs", [128, 512], F32)

            class V32:
                def __init__(self, t):
                    self.t, self.r = t, t.r

                def __getitem__(self, k):
                    return self.t[:, 0:32][k]

            pS = [V32(t) for t in pS_f]
            pDen = V32(pDen_f)
            pO = V32(pO_f)
            cnt = {"s": 0, "p": 0}
            DMA("sp", gts[:], FM[FM_GT, 0:24, SEQ:SEQ + NS], [r_FM], [gts.r])
            ACT(sg[:], gts[:], AF.Sigmoid, [gts.r], [sg.r])

            def bc4s(ap2d, P):
                return ap2d.unsqueeze(1).to_broadcast([P, 4, 8])

            def key_block_s(nk, kT_ap, kT_res, v_ap, v_res, extra, bias_fn, first, last, keep32=False):
                ps = pS[cnt["s"] % 2]
                cnt["s"] += 1
                MM(ps[0:nk, :], kT_ap, Qs[:].rearrange("p h q -> p (h q)"), True, len(extra) == 0, [kT_res, Qs.r], [ps.r])
                for ei, (l_ap, r_ap, rr) in enumerate(extra):
                    MM(ps[0:nk, :].rearrange("p (h q) -> p h q", q=8), l_ap, r_ap, False, ei == len(extra) - 1, rr, [ps.r])
                P_ = Pb[cnt["p"] % 3]
                cnt["p"] += 1
                for h in range(4):
                    dst = P32 if keep32 else P_
                    ACT(dst[0:nk, h, :], ps[0:nk, h * 8:(h + 1) * 8], AF.Exp, [ps.r] + bias_fn(h)[1], [dst.r],
                        bias=bias_fn(h)[0], scale=SCALE)
                if keep32:
                    CP("dve", P_[0:nk], P32[0:nk], [P32.r], [P_.r])
                Pf = P_[0:nk].rearrange("p h q -> p (h q)")
                MM(pDen[:], ones_b[0:nk, :], Pf, first, last, [ones_b.r, P_.r], [pDen.r])
                MM(pO[:], v_ap, Pf, first, last, [v_res, P_.r], [pO.r])

            def finish_branch_s(br, b, g, first_branch):
                TS("dve", rc[:], pDen[:], 1e-30, None, ALU.max, None, [pDen.r], [rc.r])
                RCP(rc[:], rc[:], [rc.r], [rc.r])
                for h in range(4):
                    r_ = (4 * g + h) * 3 + br
                    MM(pmisc[:, h * 8:(h + 1) * 8], tb["selrow"][:, r_ * 128:(r_ + 1) * 128], sg[:, 8 * b:8 * b + 8],
                       True, True, [tb["selrow"].r, sg.r], [pmisc.r])
                TT("dve", fac[:], rc[:], pmisc[:, 0:32], ALU.mult, [rc.r, pmisc.r], [fac.r])
                if first_branch:
                    TT("dve", acc[:], pO[:], fac[:], ALU.mult, [pO.r, fac.r], [acc.r])
                else:
                    TT("dve", tmpo[:], pO[:], fac[:], ALU.mult, [pO.r, fac.r], [tmpo.r])
                    TT("dve", acc[:], acc[:], tmpo[:], ALU.add, [tmpo.r], [acc.r])

            def compress_s(kv, srcT, dst, is_k):
                TT("dve", srcT[:].rearrange("p (n l) -> p n l", l=64), srcT[:].rearrange("p (n l) -> p n l", l=64),
                   peT[kv][:].unsqueeze(1).to_broadcast([128, 128, 64]), ALU.add, [peT[kv].r], [srcT.r])
                sv = srcT[:].rearrange("p (n l) -> p n l", l=64)
                for hh in range(2):
                    for l in range(64):
                        MM(pmisc[:, hh * 128:(hh + 1) * 128], w1[kv][:, l, hh * 128:(hh + 1) * 128], sv[:, :, l],
                           l == 0, l == 63, [w1[kv].r, srcT.r], [pmisc.r])
                ACT(hidT[:].rearrange("p a n -> p (a n)"), pmisc[:, 0:256], AF.Gelu, [pmisc.r], [hidT.r])
                for hh in range(2):
                    if is_k:
                        MM(pmisc[:, 256:384], w2[kv][:, hh, :], hidT[:, hh, :], hh == 0, hh == 1, [w2[kv].r, hidT.r], [pmisc.r])
                    else:
                        MM(pmisc[:, 256:384], hidT[:, hh, :], w2[kv][:, hh, :], hh == 0, hh == 1, [w2[kv].r, hidT.r], [pmisc.r])
                CP("dve", dst[:], pmisc[:, 256:384], [pmisc.r], [dst.r])

            import os as _os
            S.barrier()
            for b in range(int(_os.environ.get("KB", "4"))):
                tb0 = SEQ + 8 * b
                for g in range(int(_os.environ.get("KG", "2"))):
                    if ksub < 1:
                        continue
                    for pg in range(64):
                        G_ = pgb[pg % 2]
                        if _os.environ.get("KPLAIN"):
                            DMA("sp", G_[:, :], nsa_pool[pg * 128:(pg + 1) * 128, :], [], [G_.r])
                        else:
                            S.dma("pool", (lambda o, ix: lambda e: e.indirect_dma_start(
                                out=o, out_offset=None, in_=nsa_pool, in_offset=bass.IndirectOffsetOnAxis(ap=ix, axis=0)))(
                                G_[:, :], pidx[:, b * 64 + pg:b * 64 + pg + 1]), reads=[pidx.r], writes=[G_.r])
                        Gb = gbb[pg % 2]
                        CP("dve", Gb[:], G_[:, :].rearrange("p (k gg d) -> p k gg d", k=4, gg=2)[:, :, g, :], [G_.r], [Gb.r])
                        pt_ = pT3[pg % 2]
                        for ki in range(3):
                            TR(pt_[:, ki, :], Gb[:, ki, :], ident[:], [Gb.r, ident.r], [pt_.r])
                        ce = "act" if pg % 2 == 0 else "dve"
                        CP(ce, kcT[:, pg * 128:(pg + 1) * 128], pt_[:, 0, :], [pt_.r], [kcT.r])
                        CP(ce, vcT[:, pg * 128:(pg + 1) * 128], pt_[:, 1, :], [pt_.r], [vcT.r])
                        CP(ce, ksT[:, pg * 128:(pg + 1) * 128], pt_[:, 2, :], [pt_.r], [ksT.r])
                        CP("act" if pg % 2 == 1 else "dve", vsl[:, pg, :], Gb[:, 3, :], [Gb.r], [vsl.r])
                    if ksub < 2:
                        continue
                    compress_s(0, kcT, KcTs, True)
                    compress_s(1, vcT, Vcs, False)
                    if ksub < 3:
                        continue
                    for blk in range(4):
                        W_ = wcb[blk % 2]
                        DMA("sp", W_[:], win_cache[b, blk * 128:(blk + 1) * 128, :], [], [W_.r])
                        CP("dve", wkb[:], W_[:, g * 128:(g + 1) * 128], [W_.r], [wkb.r])
                        TR(pT3[blk % 2][:, 0, :], wkb[:], ident[:], [wkb.r, ident.r], [pT3[blk % 2].r])
                        CP("act", kwT[:, blk, :], pT3[blk % 2][:, 0, :], [pT3[blk % 2].r], [kwT.r])
                        CP("dve", vwl[:, blk, :], W_[:, 256 + g * 128:256 + (g + 1) * 128], [W_.r], [vwl.r])
                    DMA("sp", knT[:, 0, :], FM[FM_KS + g, :, tb0:tb0 + 8], [r_FM], [knT.r])
                    DMA("sp", knT[:, 1, :], FM[FM_KW + g, :, tb0:tb0 + 8], [r_FM], [knT.r])
                    DMA("sp", vnl[:], TMV[tb0:tb0 + 8, :], [r_TMV], [vnl.r])
                    DMA("sp", Qs[:], FM[FM_Q + 4 * g:FM_Q + 4 * g + 4, :, tb0:tb0 + 8].rearrange("c p t -> p c t"),
                        [r_FM], [Qs.r])
                    key_block_s(128, KcTs[:], KcTs.r, Vcs[:], Vcs.r, [],
                                lambda h: (tb["cbias_s"][:, 4 * g + h:4 * g + h + 1], [tb["cbias_s"].r]),
                                True, True, keep32=True)
                    finish_branch_s(0, b, g, True)
                    TT("dve", P32[:].rearrange("p h q -> p (h q)"), P32[:].rearrange("p h q -> p (h q)"), rc[:],
                       ALU.mult, [rc.r], [P32.r])
                    RED(impT[:], P32[:].rearrange("p h q -> p q h"), ALU.add, [P32.r], [impT.r])
                    TR(pmisc[0:8, 384:512], impT[:], identF[:], [impT.r, identF.r], [pmisc.r])
                    TT("dve", score[:], pmisc[0:8, 384:512], tb["scA_s"][:], ALU.mult, [pmisc.r, tb["scA_s"].r], [score.r])
                    TT("dve", score[:], score[:], tb["scB_s"][:], ALU.add, [tb["scB_s"].r], [score.r])
                    S.op("dve", lambda e: e.max(out=m16[:, 0:8], in_=score[:]), reads=[score.r], writes=[m16.r])
                    S.op("dve", lambda e: e.match_replace(out=wk[:], in_to_replace=m16[:, 0:8], in_values=score[:],
                                                          imm_value=-1e30), reads=[score.r, m16.r], writes=[wk.r])
                    S.op("dve", lambda e: e.max(out=m16[:, 8:16], in_=wk[:]), reads=[wk.r], writes=[m16.r])
                    TS("dve", m16[:, 14:15], m16[:, 14:15], 0.0, None, ALU.max, None, [m16.r], [m16.r])
                    TS("dve", negs[:], score[:], m16[:, 14:15], -30000.0, ALU.is_lt, ALU.mult, [score.r, m16.r], [negs.r])
                    TR(pmisc_b[:, 0:8], negs[:], ident[0:8, 0:8], [negs.r, ident.r], [pmisc_b.r])
                    CP("dve", negsT[:], pmisc_b[:, 0:8], [pmisc_b.r], [negsT.r])
                    if ksub < 4:
                        continue
                    for pg in range(64):
                        dlt = 64 - pg
                        key_block_s(128, ksT[:, pg * 128:(pg + 1) * 128], ksT.r, vsl[:, pg, :], vsl.r,
                                    [(tb["esel"][:, pg * 128:(pg + 1) * 128], bc4s(negsT[:], 128), [tb["esel"].r, negsT.r])],
                                    lambda h, dlt=dlt: (tb["ab"][:, dlt * 8 + 4 * g + h:dlt * 8 + 4 * g + h + 1], [tb["ab"].r]),
                                    pg == 0, False)
                    if ksub < 5:
                        continue
                    key_block_s(8, knT[:, 0, :], knT.r, vnl[:, g * 128:(g + 1) * 128], vnl.r,
                                [(ident[0:8, 0:8], bc4s(tb["tri_lo"][0:8, 0:8], 8), [ident.r, tb["tri_lo"].r])],
                                lambda h: (tb["ab"][0:8, 4 * g + h:4 * g + h + 1], [tb["ab"].r]), False, True)
                    finish_branch_s(1, b, g, False)
                    if ksub < 6:
                        continue
                    for blk in range(4):
                        dlt = 4 - blk
                        extra = []
                        if blk == 0:
                            extra.append((ident[:], bc4s(tb["tri_hi"][:, 0:8], 128), [ident.r, tb["tri_hi"].r]))
                        key_block_s(128, kwT[:, blk, :], kwT.r, vwl[:, blk, :], vwl.r, extra,
                                    lambda h, dlt=dlt: (tb["ab"][:, dlt * 8 + 4 * g + h:dlt * 8 + 4 * g + h + 1], [tb["ab"].r]),
                                    blk == 0, False)
                    key_block_s(8, knT[:, 1, :], knT.r, vnl[:, 256 + g * 128:256 + (g + 1) * 128], vnl.r,
                                [(ident[0:8, 0:8], bc4s(tb["tri_lo"][0:8, 0:8], 8), [ident.r, tb["tri_lo"].r])],
                                lambda h: (tb["ab"][0:8, 4 * g + h:4 * g + h + 1], [tb["ab"].r]), False, True)
                    finish_branch_s(2, b, g, False)
                    CP("act", attall[:, g, :, 8 * b:8 * b + 8], acc[:].rearrange("p (h q) -> p h q", q=8), [acc.r], [attall.r])
            for g in range(2):
                DMA("sp", ATTT[4 * g:4 * g + 4, :, SEQ:SEQ + NS].rearrange("c p t -> p c t"), attall[:, g, :, :],
                    [attall.r], [r_ATTT])
            end_phase()

        def phase_outproj(w_out_dram, AT, r_AT, tiles_x, Xo, r_Xo):
            with ExitStack() as ph:
                Wo = mk(ph, "sb", "Wo", [128, 16, D], BF16)
                wv = w_out_dram.rearrange("(c p) n -> p c n", p=128)
                for c0 in range(0, 16, 4):
                    DMA("pool", Wo[:, c0:c0 + 4, :], wv[:, c0:c0 + 4, :], [], [Wo.r])
                A = [mk(ph, "sb", "A", [128, 16, 128], BF16) for _ in range(2)]
                xr = [mk(ph, "sb", "xr", [128, D], F32) for _ in range(2)]
                x1 = [mk(ph, "sb", "x1", [128, D], F32) for _ in range(2)]
                po = [mk(ph, "ps", "po", [128, 512], F32) for _ in range(4)]
                for ti, (src, P, rds, tok0) in enumerate(tiles_x):
                    i = ti % 2
                    DMA("sp", A[i][:, :, 0:P], AT[:, :, tok0:tok0 + P].rearrange("c p t -> p c t"), [r_AT], [A[i].r])
                    DMA("sp", xr[i][0:P, :], src, rds, [xr[i].r])
                    for n in range(4):
                        for c in range(16):
                            MM(po[n][0:P, :], A[i][:, c, 0:P], Wo[:, c, n * 512:(n + 1) * 512], c == 0, c == 15,
                               [A[i].r, Wo.r], [po[n].r])
                        TT("dve", x1[i][0:P, n * 512:(n + 1) * 512], po[n][0:P, :], xr[i][0:P, n * 512:(n + 1) * 512],
                           ALU.add, [po[n].r, xr[i].r], [x1[i].r])
                    DMA("pool", Xo[tok0:tok0 + P, :], x1[i][0:P, :], [x1[i].r], [r_Xo])
                end_phase()

        X1 = dscr("X1", [TOK, D], F32)
        r_X1 = Res("X1")
        if stage >= 1:
            phase_outproj(w_out_ab, ATTT, r_ATTT, TILES0, X1, r_X1)
        XN1T = dscr("XN1T", [KC, 128, TOK], BF16)
        r_XN1T = Res("XN1T")
        TILES1 = [(X1[t0:t0 + P, :], P, [r_X1], t0) for (_, P, _, t0) in TILES0]
        if stage >= 1:
            phase_norm(TILES1, nwf_in[0], XN1T, r_XN1T)

        UT = dscr("UT", [128, 128, 16, 128], BF16)
        VB = dscr("VB", [16384, D], BF16)
        r_UT, r_VB = Res("UT"), Res("VB")

        def phase_peer(layer, XNT, r_XNT, Xin, r_Xin, Xout, r_Xout, ntok, tag):
            ntile = (ntok + 127) // 128
            tiles = [(t * 128, min(128, ntok - t * 128)) for t in range(ntile)]
            sups = [(s0, min(512, ntok - s0)) for s0 in range(0, ntok, 512)]
            QPT = dscr("QPT" + tag, [16, 128, ntok], BF16)
            WT = dscr("WT" + tag, [128, 128, ntok], BF16)
            GT = dscr("GT" + tag, [128, 128, 512], BF16)
            r_QPT, r_WT, r_GT = Res("QPT"), Res("WT"), Res("GT")
            with ExitStack() as ph:
                ul = [mk(ph, "sb", "ul", [128, D], BF16) for _ in range(2)]
                ut = [mk(ph, "sb", "ut", [128, 16, 128], BF16) for _ in range(2)]
                vl = [mk(ph, "sb", "vl", [128, D], BF16) for _ in range(2)]
                tp = [mk(ph, "ps", "tp", [128, 8, 128], BF16) for _ in range(4)]
                for c in range(128):
                    i = c % 2
                    DMA("pool", ul[i][:], peer_u[layer, c * 128:(c + 1) * 128, :], [], [ul[i].r])
                    for hlf in range(2):
                        tpp = tp[(2 * c + hlf) % 4]
                        for k in range(8):
                            kk = hlf * 8 + k
                            TR(tpp[:, k, :], ul[i][:, kk * 128:(kk + 1) * 128], ident[:], [ul[i].r, ident.r], [tpp.r])
                        CP("act" if hlf == 0 else "dve", ut[i][:, hlf * 8:(hlf + 1) * 8, :], tpp[:], [tpp.r], [ut[i].r])
                    DMA("sp", UT[c], ut[i][:], [ut[i].r], [r_UT])
                    DMA("pool", vl[i][:], peer_v[layer, c * 128:(c + 1) * 128, :], [], [vl[i].r])
                    DMA("sp", VB[c * 128:(c + 1) * 128, :], vl[i][:], [vl[i].r], [r_VB])
                end_phase()
            if stage < 3:
                return
            with ExitStack() as ph:
                Wq = mk(ph, "sb", "Wq", [128, 16, D], BF16)
                wv = peer_wq[layer].rearrange("(k p) n -> p k n", p=128)
                for k0 in range(0, 16, 4):
                    DMA("pool", Wq[:, k0:k0 + 4, :], wv[:, k0:k0 + 4, :], [], [Wq.r])
                XT5 = [mk(ph, "sb", "XT5", [128, 16, 512], BF16) for _ in range(2)]
                qs = [mk(ph, "sb", "qs", [128, 4, 512], BF16) for _ in range(2)]
                pq = [mk(ph, "ps", "pq", [128, 512], F32) for _ in range(4)]
                ev = 0
                for si, (s0, n) in enumerate(sups):
                    X = XT5[si % 2]
                    DMA("sp", X[:, :, 0:n], XNT[:, :, s0:s0 + n].rearrange("k p t -> p k t"), [r_XNT], [X.r])
                    for c4 in range(4):
                        Q_ = qs[(si * 4 + c4) % 2]
                        for cc in range(4):
                            c = c4 * 4 + cc
                            P_ = pq[(c + si) % 4]
                            for k in range(16):
                                MM(P_[:, 0:n], Wq[:, k, c * 128:(c + 1) * 128], X[:, k, 0:n], k == 0, k == 15,
                                   [Wq.r, X.r], [P_.r])
                            CP("act" if ev % 2 == 0 else "dve", Q_[:, cc, 0:n], P_[:, 0:n], [P_.r], [Q_.r])
                            ev += 1
                        DMA("pool", QPT[c4 * 4:c4 * 4 + 4, :, s0:s0 + n].rearrange("c p t -> p c t"), Q_[:, :, 0:n],
                            [Q_.r], [r_QPT])
                end_phase()
            if stage < 4:
                return
            with ExitStack() as ph:
                kld = mk(ph, "sb", "kld", [128, 128], BF16)
                kT = [mk(ph, "sb", "kT", [128, 128], BF16) for _ in range(2)]
                ptr = [mk(ph, "ps", "ptr", [128, 8, 128], BF16) for _ in range(2)]
                psc = [mk(ph, "ps", "psc", [128, 4, 128], F32) for _ in range(4)]
                for hf, kd in enumerate((peer_k1, peer_k2)):
                    DMA("pool", kld[:], kd[layer], [], [kld.r])
                    TR(ptr[0][:, 0, :], kld[:], ident[:], [kld.r, ident.r], [ptr[0].r])
                    CP("dve", kT[hf][:], ptr[0][:, 0, :], [ptr[0].r], [kT[hf].r])
                qT = [mk(ph, "sb", "qT", [128, 16, 128], BF16) for _ in range(2)]
                sc = mk(ph, "sb", "sc", [128, 16, 128], F32)
                v16 = mk(ph, "sb", "v16", [128, 16, 16], F32)
                wk1 = mk(ph, "sb", "wk1", [128, 128], F32)
                cand = mk(ph, "sb", "cand", [128, 8, 256], F32)
                wk2 = mk(ph, "sb", "wk2", [128, 256], F32)
                cv = mk(ph, "sb", "cv", [128, 8, 16], F32)
                ce = mk(ph, "sb", "ce", [128, 8, 16], F32)
                stt = mk(ph, "sb", "stt", [128, 32], F32)
                A32 = [mk(ph, "sb", "A32", [128, 32, 128], F32) for _ in range(2)]
                Eb = [mk(ph, "sb", "Eb", [128, 32 * 128], BF16) for _ in range(2)]
                Wacc = mk(ph, "sb", "Wacc", [128, 32 * 128], F32)
                Wtmp = mk(ph, "sb", "Wtmp", [128, 32 * 128], BF16)
                Wb = mk(ph, "sb", "Wb", [128, 32, 128], BF16)
                wT = mk(ph, "sb", "wT", [128, 32, 128], BF16)
                MS("dve", Wb[:], 0.0, [Wb.r])
                ab = 0
                for ti, (t0, P) in enumerate(tiles):
                    Q_ = qT[ti % 2]
                    DMA("sp", Q_[:, :, 0:P], QPT[:, :, t0:t0 + P].rearrange("c p t -> p c t"), [r_QPT], [Q_.r])
                    for c in range(16):
                        pb = psc[c // 4]
                        MM(pb[0:P, c % 4, :], Q_[:, c, 0:P], kT[c % 2][:], True, True, [Q_.r, kT[c % 2].r], [pb.r])
                        if c % 4 == 3:
                            CP("act", sc[0:P, c - 3:c + 1, :], pb[0:P], [pb.r], [sc.r])
                    for c in range(16):
                        S.op("dve", (lambda o, i_: lambda e: e.max(out=o, in_=i_))(v16[0:P, c, 0:8], sc[0:P, c, :]),
                             reads=[sc.r], writes=[v16.r])
                        S.op("dve", (lambda o, m, i_: lambda e: e.match_replace(out=o, in_to_replace=m, in_values=i_,
                                                                                imm_value=-1e30))(
                            wk1[0:P, :], v16[0:P, c, 0:8], sc[0:P, c, :]), reads=[sc.r, v16.r], writes=[wk1.r])
                        S.op("dve", (lambda o, i_: lambda e: e.max(out=o, in_=i_))(v16[0:P, c, 8:16], wk1[0:P, :]),
                             reads=[wk1.r], writes=[v16.r])
                    v4 = v16[0:P].rearrange("p (h t) a -> p h t a", t=2)
                    TT("dve", cand[0:P].rearrange("p h (a b) -> p h a b", b=16),
                       v4[:, :, 0, :].unsqueeze(3).to_broadcast([P, 8, 16, 16]),
                       v4[:, :, 1, :].unsqueeze(2).to_broadcast([P, 8, 16, 16]), ALU.add, [v16.r], [cand.r])
                    for h in range(8):
                        S.op("dve", (lambda o, i_: lambda e: e.max(out=o, in_=i_))(cv[0:P, h, 0:8], cand[0:P, h, :]),
                             reads=[cand.r], writes=[cv.r])
                        S.op("dve", (lambda o, m, i_: lambda e: e.match_replace(out=o, in_to_replace=m, in_values=i_,
                                                                                imm_value=-1e30))(
                            wk2[0:P, :], cv[0:P, h, 0:8], cand[0:P, h, :]), reads=[cand.r, cv.r], writes=[wk2.r])
                        S.op("dve", (lambda o, i_: lambda e: e.max(out=o, in_=i_))(cv[0:P, h, 8:16], wk2[0:P, :]),
                             reads=[wk2.r], writes=[cv.r])
                    TT("dve", ce[0:P], cv[0:P], cv[0:P, :, 0:1].to_broadcast([P, 8, 16]), ALU.subtract, [cv.r], [ce.r])
                    ACT(ce[0:P], ce[0:P], AF.Exp, [ce.r], [ce.r])
                    RED(stt[0:P, 0:8], ce[0:P], ALU.add, [ce.r], [stt.r])
                    ACT(stt[0:P, 8:16], stt[0:P, 0:8], AF.Ln, [stt.r], [stt.r])
                    TT("dve", stt[0:P, 16:24], stt[0:P, 8:16], cv[0:P, :, 0], ALU.add, [stt.r, cv.r], [stt.r])
                    TS("dve", stt[0:P, 16:24], stt[0:P, 16:24], -1.0, None, ALU.mult, None, [stt.r], [stt.r])
                    for qtr in range(4):
                        for h in range(8):
                            A_ = A32[ab % 2]
                            E_ = Eb[ab % 2]
                            ab += 1
                            TT("pool", A_[0:P],
                               sc[0:P, 2 * h, qtr * 32:(qtr + 1) * 32].unsqueeze(2).to_broadcast([P, 32, 128]),
                               sc[0:P, 2 * h + 1, :].unsqueeze(1).to_broadcast([P, 32, 128]), ALU.add, [sc.r], [A_.r])
                            ACT(E_[0:P, :], A_[0:P].rearrange("p a b -> p (a b)"), AF.Exp, [A_.r, stt.r], [E_.r],
                                bias=stt[0:P, 16 + h:17 + h])
                            dst = Wacc if h == 0 else Wtmp
                            S.op("dve", (lambda o, a, sc_, e_: lambda e: e.scalar_tensor_tensor(
                                out=o, in0=a, scalar=sc_, in1=e_, op0=ALU.is_ge, op1=ALU.mult))(
                                dst[0:P, :], A_[0:P].rearrange("p a b -> p (a b)"), cv[0:P, h, 15:16], E_[0:P, :]),
                                reads=[A_.r, cv.r, E_.r], writes=[dst.r])
                            if h > 0:
                                TT("dve", Wacc[0:P, :], Wacc[0:P, :], Wtmp[0:P, :], ALU.add, [Wtmp.r], [Wacc.r])
                        CP("act", Wb[0:P].rearrange("p a b -> p (a b)"), Wacc[0:P, :], [Wacc.r], [Wb.r])
                        for a8 in range(4):
                            pt = ptr[a8 % 2]
                            for a in range(8):
                                TR(pt[:, a, :], Wb[:, a8 * 8 + a, :], ident[:], [Wb.r, ident.r], [pt.r])
                            CP("act" if a8 % 2 == 0 else "dve", wT[:, a8 * 8:(a8 + 1) * 8, :], pt[:], [pt.r], [wT.r])
                        DMA("pool", WT[qtr * 32:(qtr + 1) * 32, :, t0:t0 + P].rearrange("c p t -> p c t"),
                            wT[:, :, 0:P], [wT.r], [r_WT])
                end_phase()
            if stage < 5:
                return
            with ExitStack() as ph:
                XT5 = mk(ph, "sb", "XT5", [128, 16, 512], BF16)
                utl = [mk(ph, "sb", "utl", [128, 16, 128], BF16) for _ in range(3)]
                wtl = [mk(ph, "sb", "wtl", [128, 512], BF16) for _ in range(3)]
                hg = [mk(ph, "sb", "hg", [128, 512], BF16) for _ in range(2)]
                gt = [mk(ph, "sb", "gt", [128, 512], BF16) for _ in range(3)]
                vb = [mk(ph, "sb", "vb", [128, 512], BF16) for _ in range(3)]
                gl = [mk(ph, "sb", "gl", [128, 512], BF16) for _ in range(3)]
                xin = [mk(ph, "sb", "xin", [128, 512], F32) for _ in range(2)]
                xo = [mk(ph, "sb", "xo", [128, 512], F32) for _ in range(2)]
                pH = [mk(ph, "ps", "pH", [128, 512], F32) for _ in range(2)]
                pO = [mk(ph, "ps", "pO", [128, 512], F32) for _ in range(4)]
                xc = 0
                for (s0, n) in sups:
                    DMA("sp", XT5[:, :, 0:n], XNT[:, :, s0:s0 + n].rearrange("k p t -> p k t"), [r_XNT], [XT5.r])
                    for c in range(128):
                        U_ = utl[c % 3]
                        W_ = wtl[c % 3]
                        DMA("sp", U_[:], UT[c], [r_UT], [U_.r])
                        DMA("sp", W_[:, 0:n], WT[c, :, s0:s0 + n], [r_WT], [W_.r])
                        PH = pH[c % 2]
                        for k in range(16):
                            MM(PH[:, 0:n], U_[:, k, :], XT5[:, k, 0:n], k == 0, k == 15, [U_.r, XT5.r], [PH.r])
                        H_ = hg[c % 2]
                        ACT(H_[:, 0:n], PH[:, 0:n], AF.Gelu, [PH.r], [H_.r])
                        G_ = gt[c % 3]
                        TT("dve", G_[:, 0:n], H_[:, 0:n], W_[:, 0:n], ALU.mult, [H_.r, W_.r], [G_.r])
                        DMA("pool", GT[c, :, 0:n], G_[:, 0:n], [G_.r], [r_GT])
                    S.barrier()
                    nsub = (n + 127) // 128
                    for dg in range(4):
                        for c in range(128):
                            V_ = vb[c % 3]
                            L_ = gl[c % 3]
                            DMA("sp", V_[:], VB[c * 128:(c + 1) * 128, dg * 512:(dg + 1) * 512], [r_VB], [V_.r])
                            DMA("sp", L_[:, 0:n], GT[c, :, 0:n], [r_GT], [L_.r])
                            for sub in range(nsub):
                                m = min(128, n - sub * 128)
                                MM(pO[sub][0:m, :], L_[:, sub * 128:sub * 128 + m], V_[:], c == 0, c == 127,
                                   [L_.r, V_.r], [pO[sub].r])
                        for sub in range(nsub):
                            m = min(128, n - sub * 128)
                            tk = s0 + sub * 128
                            XI = xin[xc % 2]
                            XO = xo[xc % 2]
                            xc += 1
                            DMA("sp", XI[0:m, :], Xin[tk:tk + m, dg * 512:(dg + 1) * 512], [r_Xin], [XI.r])
                            TT("dve", XO[0:m, :], pO[sub][0:m, :], XI[0:m, :], ALU.add, [pO[sub].r, XI.r], [XO.r])
                            DMA("pool", Xout[tk:tk + m, dg * 512:(dg + 1) * 512], XO[0:m, :], [XO.r], [r_Xout])
                end_phase()

        X2 = dscr("X2", [TOK, D], F32)
        r_X2 = Res("X2")
        if stage >= 2:
            phase_peer(0, XN1T, r_XN1T, X1, r_X1, X2, r_X2, TOK, "a")

        if stage < 6:
            S.final_wait("pool", out_res)
            S.emit()
            return nc
        NOWN = 1024 + NS
        XN2T = dscr("XN2T", [KC, 128, TOK], BF16)
        r_XN2T = Res("XN2T")
        TILES2 = [(X2[t0:t0 + P, :], P, [r_X2], t0) for (_, P, _, t0) in TILES0]
        phase_norm(TILES2, nw_in1, XN2T, r_XN2T)
        FK = dscr("FK", [4, 128, TOK], BF16)
        TMV1 = dscr("TMV1", [TOK, 512], BF16)
        LF = dscr("LF", [TOK, 16], F32)
        r_FK, r_TMV1, r_LF = Res("FK"), Res("TMV1"), Res("LF")
        C_Q, C_K, C_V, C_F = 0, 2048, 2560, 3072
        with ExitStack() as ph:
            wv = w_in_c.rearrange("(k p) c -> p k c", p=128)
            Wk = mk(ph, "sb", "Wk", [128, KC, 1024], BF16)
            Wf = mk(ph, "sb", "Wf", [128, KC, 16], BF16)
            for k0 in range(0, KC, 4):
                DMA("pool", Wk[:, k0:k0 + 4, :], wv[:, k0:k0 + 4, C_K:C_K + 1024], [], [Wk.r])
            DMA("pool", Wf[:], wv[:, :, C_F:C_F + 16], [], [Wf.r])
            bfb = mk(ph, "sb", "bfb", [128, 16], F32)
            DMA("sp", bfb[:], b_forget.partition_broadcast(128), [], [bfb.r])
            XTs = [mk(ph, "sb", "XTs", [128, KC, 512], BF16) for _ in range(2)]
            stg_f = [mk(ph, "sb", "stgf", [128, 512], F32) for _ in range(2)]
            stg_b = [mk(ph, "sb", "stgb", [128, 4, 512], BF16) for _ in range(2)]
            lf = [mk(ph, "sb", "lf", [128, 16], F32) for _ in range(2)]
            pp = [mk(ph, "ps", "pp", [128, 512], F32) for _ in range(4)]
            pf = mk(ph, "ps", "pf", [128, 16], F32)
            SUP = [(s * 512, 512) for s in range(8)] + [(SEQ, NS)]
            pc = 0
            for si, (t0, n) in enumerate(SUP):
                X = XTs[si % 2]
                DMA("sp", X[:, :, 0:n], XN2T[:, :, t0:t0 + n].rearrange("k p t -> p k t"), [r_XN2T], [X.r])
                SB_ = stg_b[si % 2]
                for g in range(4):
                    P_ = pp[pc % 4]
                    pc += 1
                    for k in range(KC):
                        MM(P_[:, 0:n], Wk[:, k, g * 128:(g + 1) * 128], X[:, k, 0:n], k == 0, k == KC - 1,
                           [Wk.r, X.r], [P_.r])
                    CP("act" if g % 2 == 0 else "dve", SB_[:, g, 0:n], P_[:, 0:n], [P_.r], [SB_.r])
                DMA("pool", FK[:, :, t0:t0 + n].rearrange("c p t -> p c t"), SB_[:, :, 0:n], [SB_.r], [r_FK])
                nsub = (n + 127) // 128
                for sub in range(nsub):
                    m = min(128, n - sub * 128)
                    tok = t0 + sub * 128
                    for kvh in range(2):
                        P_ = pp[pc % 4]
                        pc += 1
                        for k in range(KC):
                            MM(P_[0:m, :], X[:, k, sub * 128:sub * 128 + m], Wk[:, k, kvh * 512:(kvh + 1) * 512],
                               k == 0, k == KC - 1, [Wk.r, X.r], [P_.r])
                        SF = stg_f[pc % 2]
                        CP("act", SF[0:m, :], P_[0:m, :], [P_.r], [SF.r])
                        if n == 512:
                            ODMA("pool", o_pfkv[tok:tok + m, kvh * 512:(kvh + 1) * 512], SF[0:m, :], [SF.r])
                        else:
                            ODMA("pool", o_sfkv[0:m, kvh * 512:(kvh + 1) * 512], SF[0:m, :], [SF.r])
                        if kvh == 1:
                            SBv = stg_b[(si + 1) % 2]
                            CP("dve", SBv[0:m, sub, :], SF[0:m, :], [SF.r], [SBv.r])
                            DMA("pool", TMV1[tok:tok + m, :], SBv[0:m, sub, :], [SBv.r], [r_TMV1])
                    for k in range(KC):
                        MM(pf[0:m, :], X[:, k, sub * 128:sub * 128 + m], Wf[:, k, :], k == 0, k == KC - 1,
                           [Wf.r, X.r], [pf.r])
                    L_ = lf[(si * 4 + sub) % 2]
                    TT("dve", L_[0:m, :], pf[0:m, :], bfb[0:m, :], ALU.add, [pf.r, bfb.r], [L_.r])
                    ACT(L_[0:m, :], L_[0:m, :], AF.Exp, [L_.r], [L_.r], scale=-1.0)
                    TS("dve", L_[0:m, :], L_[0:m, :], 1.0, None, ALU.add, None, [L_.r], [L_.r])
                    ACT(L_[0:m, :], L_[0:m, :], AF.Ln, [L_.r], [L_.r])
                    TS("dve", L_[0:m, :], L_[0:m, :], -1.0, None, ALU.mult, None, [L_.r], [L_.r])
                    if n == 512:
                        ODMA("pool", o_plogf[tok:tok + m, :], L_[0:m, :], [L_.r])
                    else:
                        ODMA("pool", o_slogf[0:m, :], L_[0:m, :], [L_.r])
                    DMA("pool", LF[tok:tok + m, :], L_[0:m, :], [L_.r], [r_LF])
            end_phase()

        X2o = dscr("X2o", [NOWN, D], F32)
        r_X2o = Res("X2o")
        with ExitStack() as ph:
            qix = mk(ph, "sb", "qix", [128, 8], I32)
            DMA("sp", qix[:], qidx_in, [], [qix.r])
            xg = [mk(ph, "sb", "xg", [128, D], F32) for _ in range(2)]
            for s_ in range(8):
                G_ = xg[s_ % 2]
                S.dma("pool", (lambda o, ix: lambda e: e.indirect_dma_start(
                    out=o, out_offset=None, in_=X2, in_offset=bass.IndirectOffsetOnAxis(ap=ix, axis=0)))(
                    G_[:, :], qix[:, s_:s_ + 1]), reads=[qix.r, r_X2], writes=[G_.r])
                DMA("sp", X2o[s_ * 128:(s_ + 1) * 128, :], G_[:], [G_.r], [r_X2o])
            DMA("sp", xg[0][0:NS, :], X2[SEQ:SEQ + NS, :], [r_X2], [xg[0].r])
            DMA("sp", X2o[1024:1024 + NS, :], xg[0][0:NS, :], [xg[0].r], [r_X2o])
            end_phase()
        TILESO = [(X2o[t * 128:(t + 1) * 128, :], 128, [r_X2o], t * 128) for t in range(8)] + \
                 [(X2o[1024:1024 + NS, :], NS, [r_X2o], 1024)]
        XNoT = dscr("XNoT", [KC, 128, NOWN], BF16)
        r_XNoT = Res("XNoT")
        phase_norm(TILESO, nw_in1, XNoT, r_XNoT)
        QoT = dscr("QoT", [16, 128, NOWN], BF16)
        r_QoT = Res("QoT")
        with ExitStack() as ph:
            wv = w_in_c.rearrange("(k p) c -> p k c", p=128)
            Wq = mk(ph, "sb", "Wq1", [128, KC, 2048], BF16)
            for k0 in range(0, KC, 4):
                DMA("pool", Wq[:, k0:k0 + 4, :], wv[:, k0:k0 + 4, 0:2048], [], [Wq.r])
            XT5 = [mk(ph, "sb", "XT5", [128, 16, 512], BF16) for _ in range(2)]
            qs = [mk(ph, "sb", "qs", [128, 4, 512], BF16) for _ in range(2)]
            pq = [mk(ph, "ps", "pq", [128, 512], F32) for _ in range(4)]
            ev = 0
            for si, (s0, n) in enumerate([(0, 512), (512, 512), (1024, NS)]):
                X = XT5[si % 2]
                DMA("sp", X[:, :, 0:n], XNoT[:, :, s0:s0 + n].rearrange("k p t -> p k t"), [r_XNoT], [X.r])
                for c4 in range(4):
                    Q_ = qs[(si * 4 + c4) % 2]
                    for cc in range(4):
                        c = c4 * 4 + cc
                        P_ = pq[(c + si) % 4]
                        for k in range(16):
                            MM(P_[:, 0:n], Wq[:, k, c * 128:(c + 1) * 128], X[:, k, 0:n], k == 0, k == 15,
                               [Wq.r, X.r], [P_.r])
                        CP("act" if ev % 2 == 0 else "dve", Q_[:, cc, 0:n], P_[:, 0:n], [P_.r], [Q_.r])
                        ev += 1
                    DMA("pool", QoT[c4 * 4:c4 * 4 + 4, :, s0:s0 + n].rearrange("c p t -> p c t"), Q_[:, :, 0:n],
                        [Q_.r], [r_QoT])
            end_phase()

        ATT1T = dscr("ATT1T", [16, 128, NOWN], BF16)
        r_ATT1T = Res("ATT1T")
        with ExitStack() as ph:
            triU = mk(ph, "sb", "triU", [128, 128], F32)
            MS("pool", triU[:], 1.0, [triU.r])
            S.op("pool", lambda e: e.affine_select(out=triU[:], in_=triU[:], pattern=[[1, 128]],
                                                   compare_op=ALU.is_ge, fill=0.0, base=0, channel_multiplier=-1),
                 reads=[triU.r], writes=[triU.r])
            onesF = mk(ph, "sb", "onesF", [128, 128], F32)
            MS("pool", onesF[:], 1.0, [onesF.r])
            lfa = mk(ph, "sb", "lfa", [128, NT, 16], F32)
            DMA("sp", lfa[:], LF[0:SEQ, :].rearrange("(j p) h -> p j h", p=128), [r_LF], [lfa.r])
            pcs = mk(ph, "ps", "pcs", [128, 512], F32)
            pbs = mk(ph, "ps", "pbs", [128, 512], F32)
            MM(pcs[:], triU[:], lfa[:].rearrange("p j h -> p (j h)"), True, True, [triU.r, lfa.r], [pcs.r])
            MM(pbs[:], onesF[:], lfa[:].rearrange("p j h -> p (j h)"), True, True, [onesF.r, lfa.r], [pbs.r])
            bs = mk(ph, "sb", "bs", [128, NT, 16], F32)
            CP("act", bs[:].rearrange("p j h -> p (j h)"), pbs[:], [pbs.r], [bs.r])
            BP = mk(ph, "sb", "BP", [128, NT, 16], F32)
            MS("dve", BP[:, 0, :], 0.0, [BP.r])
            for j in range(1, NT):
                TT("dve", BP[:, j, :], BP[:, j - 1, :], bs[:, j - 1, :], ALU.add, [bs.r], [BP.r])
            Fk = mk(ph, "sb", "Fk", [128, NT, 16], F32)
            TT("dve", Fk[:].rearrange("p j h -> p (j h)"), pcs[:], BP[:].rearrange("p j h -> p (j h)"), ALU.add,
               [pcs.r, BP.r], [Fk.r])
            oh = mk(ph, "sb", "oh", [128, 8, NT], F32)
            bm = mk(ph, "sb", "bm", [128, 8, NT], F32)
            DMA("sp", oh[:], oh_in, [], [oh.r])
            DMA("sp", bm[:], bm_in, [], [bm.r])
            dmk = mk(ph, "sb", "dmk", [128, 8 * 4 * 128], BF16)
            DMA("pool", dmk[:], dmask_in, [], [dmk.r])
            tmpB = mk(ph, "sb", "tmpB", [128, NT, 16], F32)
            Fref = mk(ph, "sb", "Fref", [128, 8, 16], F32)
            BT = mk(ph, "sb", "BT", [128, 8, NT, 16], F32)
            for s_ in range(8):
                TT("dve", tmpB[:], BP[:], oh[:, s_, :].unsqueeze(2).to_broadcast([128, NT, 16]), ALU.mult,
                   [BP.r, oh.r], [tmpB.r])
                RED(Fref[:, s_, :], tmpB[:].rearrange("p j h -> p h j"), ALU.add, [tmpB.r], [Fref.r])
                TT("dve", BT[:, s_, :, :], Fref[:, s_, :].unsqueeze(1).to_broadcast([128, NT, 16]), Fk[:],
                   ALU.subtract, [Fref.r, Fk.r], [BT.r])
                TT("dve", BT[:, s_, :, :], BT[:, s_, :, :], bm[:, s_, :].unsqueeze(2).to_broadcast([128, NT, 16]),
                   ALU.add, [bm.r], [BT.r])
            KT1 = mk(ph, "sb", "KT1", [128, 4, SEQ], BF16)
            V1 = mk(ph, "sb", "V1", [128, NT, 512], BF16)
            for g in range(4):
                DMA("sp", KT1[:, g, :], FK[g, :, 0:SEQ], [r_FK], [KT1.r])
            for j0 in range(0, NT, 8):
                DMA("sp", V1[:, j0:j0 + 8, :], TMV1[j0 * 128:(j0 + 8) * 128, :].rearrange("(j k) c -> k j c", k=128),
                    [r_TMV1], [V1.r])
            Qg = [mk(ph, "sb", "Qg1", [128, 4, 128], BF16) for _ in range(2)]
            Pb = [mk(ph, "sb", "Pb1", [128, 4, 128], BF16) for _ in range(3)]
            rc = mk(ph, "sb", "rc1", [128, 512], F32)
            attb = mk(ph, "sb", "attb1", [128, 4, 128], BF16)
            pS = [mk(ph, "ps", "pS1", [128, 512], F32) for _ in range(2)]
            pDen = mk(ph, "ps", "pDen1", [128, 512], F32)
            pO = mk(ph, "ps", "pO1", [128, 512], F32)
            kc_ = 0
            for s_ in range(8):
                m_ = s_ // 2
                jd0 = 8 * m_ + 4 * (s_ % 2)
                jmax = jd0 + 3
                for g in range(4):
                    Q = Qg[(s_ * 4 + g) % 2]
                    DMA("sp", Q[:], QoT[4 * g:4 * g + 4, :, s_ * 128:(s_ + 1) * 128].rearrange("c p t -> p c t"),
                        [r_QoT], [Q.r])
                    for j in range(jmax + 1):
                        ps = pS[kc_ % 2]
                        P_ = Pb[kc_ % 3]
                        kc_ += 1
                        dg = j >= jd0
                        MM(ps[:], KT1[:, g, j * 128:(j + 1) * 128], Q[:].rearrange("p h q -> p (h q)"), True, not dg,
                           [KT1.r, Q.r], [ps.r])
                        if dg:
                            cnd = j - jd0
                            MM(ps[:].rearrange("p (h q) -> p h q", q=128), ident[:],
                               dmk[:, (s_ * 4 + cnd) * 128:(s_ * 4 + cnd + 1) * 128].unsqueeze(1).to_broadcast([128, 4, 128]),
                               False, True, [ident.r, dmk.r], [ps.r])
                        for h in range(4):
                            ACT(P_[:, h, :], ps[:, h * 128:(h + 1) * 128], AF.Exp, [ps.r, BT.r], [P_.r],
                                bias=BT[:, s_, j, 4 * g + h:4 * g + h + 1], scale=SCALE)
                        Pf = P_[:].rearrange("p h q -> p (h q)")
                        MM(pDen[:], ones_b[:], Pf, j == 0, j == jmax, [ones_b.r, P_.r], [pDen.r])
                        MM(pO[:], V1[:, j, g * 128:(g + 1) * 128], Pf, j == 0, j == jmax, [V1.r, P_.r], [pO.r])
                    TS("dve", rc[:], pDen[:], 1e-30, None, ALU.max, None, [pDen.r], [rc.r])
                    RCP(rc[:], rc[:], [rc.r], [rc.r])
                    TT("dve", attb[:].rearrange("p h q -> p (h q)"), pO[:], rc[:], ALU.mult, [pO.r, rc.r], [attb.r])
                    DMA("pool", ATT1T[4 * g:4 * g + 4, :, s_ * 128:(s_ + 1) * 128].rearrange("c p t -> p c t"),
                        attb[:], [attb.r], [r_ATT1T])
            end_phase()

        with ExitStack() as ph:
            load_table(ph, "tri_lo")
            pidx = page_index_tile(ph)
            triS = mk(ph, "sb", "triS", [128, 128], F32)
            MS("pool", triS[:], 1.0, [triS.r])
            S.op("pool", lambda e: e.affine_select(out=triS[:], in_=triS[:], pattern=[[-1, 128]],
                                                   compare_op=ALU.is_gt, fill=0.0, base=0, channel_multiplier=1),
                 reads=[triS.r], writes=[triS.r])
            triU = mk(ph, "sb", "triUs", [128, 128], F32)
            MS("pool", triU[:], 1.0, [triU.r])
            S.op("pool", lambda e: e.affine_select(out=triU[:], in_=triU[:], pattern=[[1, 128]],
                                                   compare_op=ALU.is_ge, fill=0.0, base=0, channel_multiplier=-1),
                 reads=[triU.r], writes=[triU.r])
            onesF = mk(ph, "sb", "onesFs", [128, 128], F32)
            MS("pool", onesF[:], 1.0, [onesF.r])
            pgb = [mk(ph, "sb", "fpgb", [128, 1024], F32) for _ in range(2)]
            gbb = [mk(ph, "sb", "fgbb", [128, 8, 128], BF16) for _ in range(2)]
            kTa = mk(ph, "sb", "kTa", [128, 4, 8192], BF16)
            va = mk(ph, "sb", "va", [128, 64, 512], BF16)
            LFp = mk(ph, "sb", "LFp", [128, 64, 16], F32)
            PT = mk(ph, "sb", "PTs", [128, 64, 16], F32)
            SP = mk(ph, "sb", "SPs", [128, 64, 16], F32)
            Dk = mk(ph, "sb", "Dk", [128, 64, 16], F32)
            nlf = mk(ph, "sb", "nlf", [8, 16], F32)
            ncum = mk(ph, "sb", "ncum", [8, 16], F32)
            knT = mk(ph, "sb", "fknT", [128, 4, 8], BF16)
            vnl = mk(ph, "sb", "fvnl", [8, 512], BF16)
            Qs = mk(ph, "sb", "fQs", [128, 4, 8], BF16)
            Pb = [mk(ph, "sb", "fPb", [128, 4, 8], BF16) for _ in range(3)]
            rc = mk(ph, "sb", "frc", [128, 32], F32)
            attall = mk(ph, "sb", "fattall", [128, 4, 4, NS], BF16)
            MS("dve", attall[:], 0.0, [attall.r])
            pT = [mk(ph, "ps", "fpT", [128, 8, 128], BF16) for _ in range(2)]
            pS_f = [mk(ph, "ps", "fpS", [128, 512], F32) for _ in range(2)]
            pDen_f = mk(ph, "ps", "fpDen", [128, 512], F32)
            pO_f = mk(ph, "ps", "fpO", [128, 512], F32)
            pm = mk(ph, "ps", "fpm", [128, 512], F32)
            kb = 0
            for b in range(4):
                for pg in range(64):
                    G_ = pgb[pg % 2]
                    S.dma("pool", (lambda o, ix: lambda e: e.indirect_dma_start(
                        out=o, out_offset=None, in_=fox_pool, in_offset=bass.IndirectOffsetOnAxis(ap=ix, axis=0)))(
                        G_[:, :], pidx[:, b * 64 + pg:b * 64 + pg + 1]), reads=[pidx.r], writes=[G_.r])
                    S.dma("pool", (lambda o, ix: lambda e: e.indirect_dma_start(
                        out=o, out_offset=None, in_=foxlf_pool, in_offset=bass.IndirectOffsetOnAxis(ap=ix, axis=0)))(
                        LFp[:, pg, :], pidx[:, b * 64 + pg:b * 64 + pg + 1]), reads=[pidx.r], writes=[LFp.r])
                    Gb = gbb[pg % 2]
                    CP("dve" if pg % 2 == 0 else "act", Gb[:].rearrange("p a d -> p (a d)"), G_[:, :], [G_.r], [Gb.r])
                    pt_ = pT[pg % 2]
                    for g in range(4):
                        TR(pt_[:, g, :], Gb[:, g, :], ident[:], [Gb.r, ident.r], [pt_.r])
                    CP("act" if pg % 2 == 0 else "dve", kTa[:, :, pg * 128:(pg + 1) * 128], pt_[:, 0:4, :], [pt_.r], [kTa.r])
                    CP("dve" if pg % 2 == 0 else "act", va[:, pg, :], Gb[:, 4:8, :].rearrange("p a d -> p (a d)"), [Gb.r], [va.r])
                LFf = LFp[:].rearrange("p a h -> p (a h)")
                for hf in range(2):
                    MM(pm[:], onesF[:], LFf[:, hf * 512:(hf + 1) * 512], True, True, [onesF.r, LFp.r], [pm.r])
                    CP("dve", PT[:].rearrange("p a h -> p (a h)")[:, hf * 512:(hf + 1) * 512], pm[:], [pm.r], [PT.r])
                MS("dve", SP[:, 63, :], 0.0, [SP.r])
                for pg in range(62, -1, -1):
                    TT("dve", SP[:, pg, :], SP[:, pg + 1, :], PT[:, pg + 1, :], ALU.add, [PT.r], [SP.r])
                for hf in range(2):
                    MM(pm[:], triS[:], LFf[:, hf * 512:(hf + 1) * 512], True, True, [triS.r, LFp.r], [pm.r])
                    TT("dve", Dk[:].rearrange("p a h -> p (a h)")[:, hf * 512:(hf + 1) * 512], pm[:],
                       SP[:].rearrange("p a h -> p (a h)")[:, hf * 512:(hf + 1) * 512], ALU.add, [pm.r, SP.r], [Dk.r])
                tb0 = SEQ + 8 * b
                DMA("sp", nlf[:], LF[tb0:tb0 + 8, :], [r_LF], [nlf.r])
                MM(pm[0:8, 0:16], triU[0:8, 0:8], nlf[:], True, True, [triU.r, nlf.r], [pm.r])
                TS("dve", ncum[:], pm[0:8, 0:16], -1.0, None, ALU.mult, None, [pm.r], [ncum.r])
                DMA("sp", knT[:], FK[:, :, tb0:tb0 + 8].rearrange("c p t -> p c t"), [r_FK], [knT.r])
                DMA("sp", vnl[:], TMV1[tb0:tb0 + 8, :], [r_TMV1], [vnl.r])
                for g in range(4):
                    DMA("sp", Qs[:], QoT[4 * g:4 * g + 4, :, 1024 + 8 * b:1024 + 8 * b + 8].rearrange("c p t -> p c t"),
                        [r_QoT], [Qs.r])
                    Qf = Qs[:].rearrange("p h q -> p (h q)")
                    for j in range(65):
                        new = (j == 64)
                        nk = 8 if new else 128
                        ps = pS_f[kb % 2]
                        P_ = Pb[kb % 3]
                        kb += 1
                        if new:
                            MM(ps[0:8, 0:32], knT[:, g, :], Qf, True, False, [knT.r, Qs.r], [ps.r])
                            MM(ps[0:8, 0:32].rearrange("p (h q) -> p h q", q=8), ident[0:8, 0:8],
                               tb["tri_lo"][0:8, 0:8].unsqueeze(1).to_broadcast([8, 4, 8]), False, True,
                               [ident.r, tb["tri_lo"].r], [ps.r])
                        else:
                            MM(ps[:, 0:32], kTa[:, g, j * 128:(j + 1) * 128], Qf, True, True, [kTa.r, Qs.r], [ps.r])
                        for h in range(4):
                            bias = ncum[0:8, 4 * g + h:4 * g + h + 1] if new else Dk[:, j, 4 * g + h:4 * g + h + 1]
                            ACT(P_[0:nk, h, :], ps[0:nk, h * 8:(h + 1) * 8], AF.Exp, [ps.r, ncum.r if new else Dk.r], [P_.r],
                                bias=bias, scale=SCALE)
                        Pf = P_[0:nk].rearrange("p h q -> p (h q)")
                        MM(pDen_f[:, 0:32], ones_b[0:nk, :], Pf, j == 0, j == 64, [ones_b.r, P_.r], [pDen_f.r])
                        vv = vnl[:, g * 128:(g + 1) * 128] if new else va[:, j, g * 128:(g + 1) * 128]
                        MM(pO_f[:, 0:32], vv, Pf, j == 0, j == 64, [vnl.r if new else va.r, P_.r], [pO_f.r])
                    TS("dve", rc[:], pDen_f[:, 0:32], 1e-30, None, ALU.max, None, [pDen_f.r], [rc.r])
                    RCP(rc[:], rc[:], [rc.r], [rc.r])
                    TT("dve", attall[:, g, :, 8 * b:8 * b + 8], pO_f[:, 0:32].rearrange("p (h q) -> p h q", q=8),
                       rc[:].rearrange("p (h q) -> p h q", q=8), ALU.mult, [pO_f.r, rc.r], [attall.r])
            for g in range(4):
                DMA("sp", ATT1T[4 * g:4 * g + 4, :, 1024:1024 + NS].rearrange("c p t -> p c t"), attall[:, g, :, :],
                    [attall.r], [r_ATT1T])
            end_phase()

        if stage < 7:
            S.final_wait("pool", out_res)
            S.emit()
            return nc
        X3o = dscr("X3o", [NOWN, D], F32)
        r_X3o = Res("X3o")
        phase_outproj(w_out_c, ATT1T, r_ATT1T, TILESO, X3o, r_X3o)
        XN3oT = dscr("XN3oT", [KC, 128, NOWN], BF16)
        r_XN3oT = Res("XN3oT")
        TILES3 = [(X3o[t0:t0 + P, :], P, [r_X3o], t0) for (_, P, _, t0) in TILESO]
        phase_norm(TILES3, nwf_in[1], XN3oT, r_XN3oT)
        X4o = dscr("X4o", [NOWN, D], F32)
        r_X4o = Res("X4o")
        phase_peer(1, XN3oT, r_XN3oT, X3o, r_X3o, X4o, r_X4o, NOWN, "b")
        TILES4 = [(X4o[t0:t0 + P, :], P, [r_X4o], t0) for (_, P, _, t0) in TILESO]
        phase_norm(TILES4, None, None, None, final_out=(lambda tok0, P: o_y[tok0:tok0 + P, :], nfin_in))

        S.final_wait("pool", out_res)
        S.emit()
    return nc


_NC_CACHE = {}


def kernel(x_prompt, x_sample, cache_nsa_kv, cache_nsa_win, state_ret, cache_fox_kv, cache_fox_logf,
           page_table, norm_mix, norm_ffn, norm_final, w_in_ab, w_out_ab, cmp_pe_k, cmp_w1_k, cmp_w2_k,
           cmp_pe_v, cmp_w1_v, cmp_w2_v, ret_gn, w_in_c, b_forget, w_out_c, peer_wq, peer_k1, peer_k2,
           peer_u, peer_v, _debug_outs=()):
    f32 = np.float32
    x_prompt = np.asarray(x_prompt, f32)
    x_sample = np.asarray(x_sample, f32)
    tabs = host_tables()
    nw0 = np.ascontiguousarray(np.asarray(norm_mix, f32)[0].reshape(KC, 128).T)
    w_ab = np.ascontiguousarray(np.asarray(w_in_ab, f32)[0])
    win_c = np.asarray(cache_nsa_win, f32)[0].reshape(32, 512, 512)
    st_r = np.asarray(state_ret, f32)[0]
    gn = np.ascontiguousarray(np.asarray(ret_gn, f32)[0].reshape(1, 1024))

    cw = dict(cmp_pe_k=cmp_pe_k, cmp_pe_v=cmp_pe_v, cmp_w1_k=cmp_w1_k, cmp_w1_v=cmp_w1_v,
              cmp_w2_k=cmp_w2_k, cmp_w2_v=cmp_w2_v)
    cw = {k: np.ascontiguousarray(np.asarray(v, f32)[0]) for k, v in cw.items()}
    wo_ab = np.ascontiguousarray(np.asarray(w_out_ab, f32)[0])
    nwf = np.ascontiguousarray(np.asarray(norm_ffn, f32).reshape(2, KC, 128).transpose(0, 2, 1))
    pw = {k: np.ascontiguousarray(np.asarray(v, f32)) for k, v in
          dict(peer_wq=peer_wq, peer_k1=peer_k1, peer_k2=peer_k2, peer_u=peer_u, peer_v=peer_v).items()}
    nw1 = np.ascontiguousarray(np.asarray(norm_mix, f32)[1].reshape(KC, 128).T)
    nfin = np.ascontiguousarray(np.asarray(norm_final, f32).reshape(1, D))
    w_c = np.ascontiguousarray(np.asarray(w_in_c, f32)[0])
    bfg = np.ascontiguousarray(np.asarray(b_forget, f32)[0].reshape(1, 16))
    wo_c = np.ascontiguousarray(np.asarray(w_out_c, f32)[0])
    core_tabs = [core_tables(c) for c in range(8)]
    ptab = np.asarray(page_table, np.int32)
    nsa_pool = np.ascontiguousarray(np.asarray(cache_nsa_kv, f32)[0]).reshape(-1, 1024)
    npool = nsa_pool.shape[0] // 128
    fox_pool = np.ascontiguousarray(np.asarray(cache_fox_kv, f32)[0]).reshape(-1, 1024)
    foxlf_pool = np.ascontiguousarray(np.asarray(cache_fox_logf, f32)[0]).reshape(-1, 16)
    import os
    stage = int(os.environ.get("KSTAGE", "99"))
    ksub = int(os.environ.get("KSUB", "99"))
    nexp = 16384
    if stage < 2:
        nexp = 128
        pw["peer_u"] = np.ascontiguousarray(pw["peer_u"][:, :128])
        pw["peer_v"] = np.ascontiguousarray(pw["peer_v"][:, :128])
    key = tuple(_debug_outs) + (stage, npool, nexp, ksub)
    if key not in _NC_CACHE:
        _NC_CACHE[key] = build_nc(_debug_outs, stage, npool, nexp, ksub)
    nc = _NC_CACHE[key]
    in_maps = []
    for c in range(8):
        b = c % 2
        m = {
            "xseq": np.ascontiguousarray(x_prompt[b]),
            "xsamp": np.ascontiguousarray(x_sample[4 * c:4 * c + 4].reshape(NS, D)),
            "nw_mix0": nw0,
            "w_in_ab": w_ab,
            "win_cache": np.ascontiguousarray(win_c[4 * c:4 * c + 4]),
            "state_ret": np.ascontiguousarray(st_r[4 * c:4 * c + 4]),
            "ret_gn": gn,
            "page_table": np.ascontiguousarray(ptab[4 * c:4 * c + 4].reshape(1, 256)),
            "nsa_pool": nsa_pool, "fox_pool": fox_pool, "foxlf_pool": foxlf_pool,
            "nw_mix1": nw1, "norm_final": nfin, "w_in_c": w_c, "b_forget": bfg, "w_out_c": wo_c,
            "qidx": core_tabs[c]["qidx"], "oh": core_tabs[c]["oh"], "bm": core_tabs[c]["bm"],
            "dmask": core_tabs[c]["dmask"],
            "w_out_ab": wo_ab, "nw_ffn": nwf, "peer_wq": pw["peer_wq"], "peer_k1": pw["peer_k1"],
            "peer_k2": pw["peer_k2"], "peer_u": pw["peer_u"], "peer_v": pw["peer_v"],
            "cmp_pe_k": cw["cmp_pe_k"], "cmp_pe_v": cw["cmp_pe_v"], "cmp_w1_k": cw["cmp_w1_k"],
            "cmp_w1_v": cw["cmp_w1_v"], "cmp_w2_k": cw["cmp_w2_k"], "cmp_w2_v": cw["cmp_w2_v"],
        }
        for k, v in tabs.items():
            m["tb_" + k] = v
        in_maps.append(m)
    res = run_bass_kernel_spmd(nc, in_maps, core_ids=list(range(8)))
    R = res.results
    if _debug_outs:
        kernel.debug = R

    y_prompt = np.zeros((2, SEQ, D), f32)
    y_sample = np.zeros((32, 8, D), f32)
    if stage >= 7:
        for c in range(8):
            oy = R[c]["o_y"]
            for s_, i in enumerate(own_blocks(c)):
                y_prompt[c % 2, i * 128:(i + 1) * 128] = oy[s_ * 128:(s_ + 1) * 128]
            y_sample[4 * c:4 * c + 4] = oy[1024:1024 + NS].reshape(4, 8, D)
    p_nsa_kv = np.stack([R[b]["o_pkv"].reshape(SEQ, 4, 2, 128) for b in range(2)])[None]
    p_nsa_win = np.stack([R[b]["o_pwin"].reshape(512, 2, 2, 128) for b in range(2)])[None]
    p_ret = np.stack([R[b]["o_pret"] for b in range(2)])[None]
    p_fox_kv = np.stack([R[b]["o_pfkv"].reshape(SEQ, 2, 4, 128) for b in range(2)])[None].astype(f32)
    p_fox_logf = np.stack([R[b]["o_plogf"] for b in range(2)])[None].astype(f32)
    s_nsa_kv = np.concatenate([R[c]["o_skv"].reshape(4, 8, 4, 2, 128) for c in range(8)])[None]
    s_nsa_win = np.concatenate([R[c]["o_swin"].reshape(4, 512, 2, 2, 128) for c in range(8)])[None]
    s_ret = np.concatenate([R[c]["o_sret"] for c in range(8)])[None]
    s_fox_kv = np.concatenate([R[c]["o_sfkv"].reshape(4, 8, 2, 4, 128) for c in range(8)])[None].astype(f32)
    s_fox_logf = np.concatenate([R[c]["o_slogf"].reshape(4, 8, 16) for c in range(8)])[None].astype(f32)
    return (y_prompt, y_sample, p_nsa_kv.astype(f32), p_nsa_win.astype(f32), p_ret.astype(f32),
            p_fox_kv, p_fox_logf, s_nsa_kv.astype(f32), s_nsa_win.astype(f32), s_ret.astype(f32),
            s_fox_kv, s_fox_logf)
```
